# Optimizing a Trainium2 kernel written in Bass

```python
import jax, jax.numpy as jnp
from jax import lax
import numpy as np

D_MODEL = 1024
BATCH = 4
SEQ = 4096
DEPTH = 4

D_PLE = 256
M_HEADS = 4
M_HEAD_DIM = 128
M_WIDTH = M_HEADS * M_HEAD_DIM
M_CONV = 5
M_CHUNK = 64
N_GATE = 4 * M_HEADS
A_HEADS = 8
A_KV_HEADS = 2
A_HEAD_DIM = 64
A_WIDTH = A_HEADS * A_HEAD_DIM
A_KV_WIDTH = A_KV_HEADS * A_HEAD_DIM
A_GROUP = A_HEADS // A_KV_HEADS
WINDOW = 128
A_BLOCK = 128
MIX_WIDTH = M_WIDTH + A_WIDTH
IN_WIDTH = 4 * M_WIDTH + N_GATE + A_WIDTH + 2 * A_KV_WIDTH
D_FF = ((-(-8 * D_MODEL // 3) + 255) // 256) * 256
ALPHA = (2 * DEPTH) ** 0.25
BETA = (8 * DEPTH) ** -0.25
LN_EPS = 1e-5
MH_EPS = 1e-6

kernel_name = "hybrid_mlstm_swa_deepnorm_encoder"


def _layer_norm(x, g, b):
    xf = x.astype(jnp.float32)
    mu = xf.mean(-1, keepdims=True)
    var = jnp.mean(jnp.square(xf - mu), -1, keepdims=True)
    return ((xf - mu) * lax.rsqrt(var + LN_EPS) * g + b).astype(x.dtype)


def _centred_dwconv(x, w, b):
    pad = w.shape[0] // 2
    y = lax.conv_general_dilated(x, w[:, None, :], window_strides=(1,), padding=[(pad, pad)],
                                 dimension_numbers=("NWC", "WIO", "NWC"),
                                 feature_group_count=x.shape[-1])
    return y + b


def _mlstm_scan(q, k, v, i_pre, log_f):
    B, H, S, dh = q.shape
    L = M_CHUNK
    nc = S // L

    def chunks(t):
        return jnp.moveaxis(t.reshape(B, H, nc, L, *t.shape[3:]), 2, 0)

    lower = jnp.tril(jnp.ones((L, L), dtype=bool))

    def step(carry, xs):
        C, n, m = carry
        qc, kc, vc, ic, fc = xs
        b = jnp.cumsum(fc, axis=-1)
        dmat = b[..., :, None] - b[..., None, :] + ic[..., None, :]
        dmat = jnp.where(lower, dmat, -jnp.inf)
        inter = b + m[..., None]
        m_t = jnp.maximum(dmat.max(-1), inter)
        w_intra = jnp.exp(dmat - m_t[..., None])
        w_inter = jnp.exp(inter - m_t)
        s = jnp.einsum('bhtd,bhsd->bhts', qc, kc) * w_intra
        num = jnp.einsum('bhts,bhse->bhte', s, vc) + w_inter[..., None] * jnp.einsum('bhed,bhtd->bhte', C, qc)
        den = s.sum(-1) + w_inter * jnp.einsum('bhd,bhtd->bht', n, qc)
        h = num / jnp.maximum(jnp.abs(den), jnp.exp(-m_t))[..., None]
        g = b[..., -1]
        a = g[..., None] - b + ic
        m_new = jnp.maximum(g + m, a.max(-1))
        decay = jnp.exp(g + m - m_new)
        wa = jnp.exp(a - m_new[..., None])
        C = decay[..., None, None] * C + jnp.einsum('bhs,bhse,bhsd->bhed', wa, vc, kc)
        n = decay[..., None] * n + jnp.einsum('bhs,bhsd->bhd', wa, kc)
        return (C, n, m_new), h

    init = (jnp.zeros((B, H, dh, dh), jnp.float32), jnp.zeros((B, H, dh), jnp.float32),
            jnp.zeros((B, H), jnp.float32))
    _, h = lax.scan(step, init, (chunks(q), chunks(k), chunks(v), chunks(i_pre), chunks(log_f)))
    return jnp.moveaxis(h, 0, 2).reshape(B, H, S, dh)


def _mlstm_group(qk, v, o, gates, b_gate, conv_w, conv_b, norm_g):
    B, S, _ = v.shape
    qk = jax.nn.silu(_centred_dwconv(qk, conv_w, conv_b))
    q, k = jnp.split(qk, 2, axis=-1)

    def to_heads(t):
        return t.reshape(B, S, M_HEADS, M_HEAD_DIM).transpose(0, 2, 1, 3).astype(jnp.float32)

    q, k, vh = to_heads(q), to_heads(k) * (M_HEAD_DIM ** -0.5), to_heads(v)
    g = (gates + b_gate).astype(jnp.float32).reshape(B, S, 4, M_HEADS).transpose(2, 0, 3, 1)
    h_fwd = _mlstm_scan(q, k, vh, g[0], jax.nn.log_sigmoid(g[2]))

    def flip(t):
        return jnp.flip(t, axis=2)

    h_bwd = flip(_mlstm_scan(flip(q), flip(k), flip(vh), flip(g[1]), flip(jax.nn.log_sigmoid(g[3]))))
    h = h_fwd + h_bwd
    mu = h.mean(-1, keepdims=True)
    var = jnp.mean(jnp.square(h - mu), -1, keepdims=True)
    h = ((h - mu) * lax.rsqrt(var + MH_EPS)).transpose(0, 2, 1, 3).reshape(B, S, M_WIDTH)
    return (h * norm_g * jax.nn.sigmoid(o.astype(jnp.float32))).astype(v.dtype)


def _window_gqa(q, k, v, sink):
    B, S, _ = q.shape
    nb = S // A_BLOCK
    qb = q.reshape(B, nb, A_BLOCK, A_KV_HEADS, A_GROUP, A_HEAD_DIM)

    def band(t):
        t = t.reshape(B, nb, A_BLOCK, A_KV_HEADS, A_HEAD_DIM)
        t = jnp.pad(t, ((0, 0), (1, 1), (0, 0), (0, 0), (0, 0)))
        return jnp.concatenate([t[:, :-2], t[:, 1:-1], t[:, 2:]], axis=2)

    kb, vb = band(k), band(v)
    s = jnp.einsum('bnqhgd,bnkhd->bnhgqk', qb, kb).astype(jnp.float32) * (A_HEAD_DIM ** -0.5)
    blk = jnp.arange(nb)[:, None]
    qpos = blk * A_BLOCK + jnp.arange(A_BLOCK)[None]
    kpos = (blk - 1) * A_BLOCK + jnp.arange(3 * A_BLOCK)[None]
    rel = jnp.abs(kpos[:, None, :] - qpos[:, :, None])
    valid = (rel <= WINDOW) & (kpos[:, None, :] >= 0) & (kpos[:, None, :] < S)
    slopes = jnp.exp2(-8.0 * jnp.arange(1, A_HEADS + 1, dtype=jnp.float32) / A_HEADS)
    bias = -slopes.reshape(A_KV_HEADS, A_GROUP)[None, :, :, None, None] * rel[:, None, None].astype(jnp.float32)
    s = jnp.where(valid[:, None, None], s + bias, -jnp.inf)
    sk = sink.astype(jnp.float32).reshape(A_KV_HEADS, A_GROUP)[None, None, :, :, None, None]
    mx = jnp.maximum(s.max(-1, keepdims=True), sk)
    e = jnp.exp(s - mx)
    prob = e / (e.sum(-1, keepdims=True) + jnp.exp(sk - mx))
    out = jnp.einsum('bnhgqk,bnkhd->bnqhgd', prob.astype(v.dtype), vb)
    return out.reshape(B, S, A_WIDTH)


def _hybrid_layer(x, p_i, w_in, b_gate, conv_w, conv_b, mlstm_norm_g, attn_sink, w_out,
                  ln1_g, ln1_b, w_ffn_in, w_ffn_out, ln2_g, ln2_b, w_ple_gate, w_ple_proj):
    u = x @ w_in
    cuts = np.cumsum([2 * M_WIDTH, M_WIDTH, M_WIDTH, N_GATE, A_WIDTH, A_KV_WIDTH]).tolist()
    m_qk, m_v, m_o, m_gates, a_q, a_k, a_v = jnp.split(u, cuts, axis=-1)
    h_m = _mlstm_group(m_qk, m_v, m_o, m_gates, b_gate, conv_w, conv_b, mlstm_norm_g)
    h_a = _window_gqa(a_q, a_k, a_v, attn_sink)
    mix = jnp.concatenate([h_m, h_a], axis=-1) @ w_out
    x = _layer_norm(ALPHA * x + mix, ln1_g, ln1_b)
    gate_ff, up_ff = jnp.split(x @ w_ffn_in, 2, axis=-1)
    ffn = (jax.nn.silu(gate_ff) * up_ff) @ w_ffn_out
    ple = jax.nn.sigmoid(x @ w_ple_gate) * (p_i @ w_ple_proj)
    return _layer_norm(ALPHA * x + ffn + ple, ln2_g, ln2_b)


def setup_inputs(seed: int = 0) -> dict:
    key = jax.random.key(seed)
    ks = jax.random.split(key, 20)
    nrm = lambda k, shape: jax.random.normal(k, shape, jnp.float32)
    col_scale = np.ones((IN_WIDTH,), np.float32)
    col_scale[3 * M_WIDTH - M_WIDTH:3 * M_WIDTH] = BETA
    av0 = 4 * M_WIDTH + N_GATE + A_WIDTH + A_KV_WIDTH
    col_scale[av0:av0 + A_KV_WIDTH] = BETA
    w_in = nrm(ks[2], (DEPTH, D_MODEL, IN_WIDTH)) * (D_MODEL ** -0.5) * jnp.asarray(col_scale)
    f_lin = jnp.linspace(3.0, 6.0, M_HEADS, dtype=jnp.float32)
    b_gate = jnp.concatenate([0.1 * nrm(ks[3], (DEPTH, 2 * M_HEADS)),
                              jnp.concatenate([f_lin, f_lin])[None] + 0.1 * nrm(ks[4], (DEPTH, 2 * M_HEADS))], axis=-1)
    return {
        "x": nrm(ks[0], (BATCH, SEQ, D_MODEL)),
        "p": nrm(ks[1], (DEPTH, BATCH, SEQ, D_PLE)),
        "w_in": w_in,
        "b_gate": b_gate,
        "conv_w": nrm(ks[5], (DEPTH, M_CONV, 2 * M_WIDTH)) * (M_CONV ** -0.5),
        "conv_b": 0.01 * nrm(ks[6], (DEPTH, 2 * M_WIDTH)),
        "mlstm_norm_g": 1.0 + 0.01 * nrm(ks[7], (DEPTH, M_WIDTH)),
        "attn_sink": 0.5 * nrm(ks[8], (DEPTH, A_HEADS)),
        "w_out": nrm(ks[9], (DEPTH, MIX_WIDTH, D_MODEL)) * (MIX_WIDTH ** -0.5) * BETA,
        "ln1_g": 1.0 + 0.01 * nrm(ks[10], (DEPTH, D_MODEL)),
        "ln1_b": 0.01 * nrm(ks[11], (DEPTH, D_MODEL)),
        "w_ffn_in": nrm(ks[12], (DEPTH, D_MODEL, 2 * D_FF)) * (D_MODEL ** -0.5) * BETA,
        "w_ffn_out": nrm(ks[13], (DEPTH, D_FF, D_MODEL)) * (D_FF ** -0.5) * BETA,
        "ln2_g": 1.0 + 0.01 * nrm(ks[14], (DEPTH, D_MODEL)),
        "ln2_b": 0.01 * nrm(ks[15], (DEPTH, D_MODEL)),
        "w_ple_gate": nrm(ks[16], (DEPTH, D_MODEL, D_MODEL)) * (D_MODEL ** -0.5),
        "w_ple_proj": nrm(ks[17], (DEPTH, D_PLE, D_MODEL)) * (D_PLE ** -0.5) * BETA,
    }


def reference(x, p, w_in, b_gate, conv_w, conv_b, mlstm_norm_g, attn_sink, w_out,
              ln1_g, ln1_b, w_ffn_in, w_ffn_out, ln2_g, ln2_b, w_ple_gate, w_ple_proj):
    h = x
    for i in range(DEPTH):
        h = _hybrid_layer(h, p[i], w_in[i], b_gate[i], conv_w[i], conv_b[i], mlstm_norm_g[i],
                          attn_sink[i], w_out[i], ln1_g[i], ln1_b[i], w_ffn_in[i], w_ffn_out[i],
                          ln2_g[i], ln2_b[i], w_ple_gate[i], w_ple_proj[i])
    return h
```

```python
import bisect
import numpy as np
import concourse.bass as bass
import concourse.mybir as mybir
from concourse.bass_utils import run_bass_kernel_spmd

F32 = mybir.dt.float32
BF16 = mybir.dt.bfloat16
U8 = mybir.dt.uint8
AF = mybir.ActivationFunctionType
ALU = mybir.AluOpType
AX = mybir.AxisListType
ESZ = {F32: 4, BF16: 2, U8: 1}

D = 1024
T = 4096
NT = T // 128
DEPTH = 4
DFF = 2816
NFC = DFF // 128
ALPHA = (2 * DEPTH) ** 0.25
LN_EPS = 1e-5
MH_EPS = 1e-6
NEG = -240000.0


class IMap:
    def __init__(self):
        self.b = [0, 1 << 60]
        self.s = [[None, {}]]

    def _split(self, x):
        i = bisect.bisect_right(self.b, x) - 1
        if self.b[i] == x:
            return i
        w, r = self.s[i]
        self.b.insert(i + 1, x)
        self.s.insert(i + 1, [w, dict(r)])
        return i + 1

    def segs(self, lo, hi):
        i = self._split(lo)
        j = self._split(hi)
        return self.s[i:j]


class Op:
    __slots__ = ("eng", "fn", "waits", "dma", "idx", "ms", "rank")


class Prog:
    NS = 12

    def __init__(self, nc):
        self.nc = nc
        self.eng = {"pe": nc.tensor, "act": nc.scalar, "dve": nc.vector, "pool": nc.gpsimd, "sp": nc.sync}
        self.csem = {e: nc.alloc_semaphore("c_" + e) for e in ("pe", "act", "dve")}
        self.dsem = {q: [nc.alloc_semaphore("d_%s%d" % (q, j)) for j in range(self.NS)] for q in ("sp", "pool")}
        self.ops = []
        self.cnt = {e: 0 for e in self.eng}
        self.ndma = {"sp": 0, "pool": 0}
        self.seen = {e: {} for e in self.eng}
        self.maps = {}
        self.addr = {}

    def regions(self, ap):
        t = ap.tensor
        name = t.name
        pairs = list(ap.ap)
        off = ap.offset
        esz = ESZ[ap.dtype]
        if name in self.addr:
            space, base = self.addr[name]
            pstep = pairs[0][0]
            fo = off % pstep if pstep > 0 else off
            dims = pairs[1:]
        else:
            space, base = ("d", name), 0
            fo = off
            dims = pairs
        if space == "ps":
            hi = fo + 1 + sum((c - 1) * s for s, c in dims if c > 1 and s > 0)
            return space, [((fo * esz) // 2048 * 2048, ((hi * esz) + 2047) // 2048 * 2048)]
        dims = [(s, c) for (s, c) in dims if c > 1 and s != 0]
        if not dims:
            return space, [(base + fo * esz, base + (fo + 1) * esz)]
        inner = dims[-1]
        outer = dims[:-1]
        if inner[0] == 1:
            ilen = inner[1]
        else:
            ilen = (inner[1] - 1) * inner[0] + 1
        n = 1
        for s, c in outer:
            n *= c
        if n > 48 or space[0] == "d":
            hi = fo + ilen + sum((c - 1) * s for s, c in outer)
            return space, [(base + fo * esz, base + hi * esz)]
        starts = [fo]
        for s, c in outer:
            starts = [x + i * s for x in starts for i in range(c)]
        return space, [(base + x * esz, base + (x + ilen) * esz) for x in starts]

    _cap = None

    def capture(self, f, *a):
        self._cap = []
        f(*a)
        out, self._cap = self._cap, None
        return out

    def op(self, eng, fn, reads=(), writes=(), dma=None):
        if self._cap is not None:
            self._cap.append((eng, fn, list(reads), list(writes), dma))
            return None
        o = Op()
        o.eng, o.fn, o.ms, o.rank = eng, fn, False, 0
        is_dma = (eng in ("sp", "pool")) if dma is None else dma
        self.cnt[eng] += 1
        o.idx = self.cnt[eng]
        deps = []
        rsegs, wsegs = [], []
        writes = list(writes) + [ap for ap in reads if ap.tensor.name == "ps"]
        reads = [ap for ap in reads if ap.tensor.name != "ps"]
        for ap in reads:
            sp, runs = self.regions(ap)
            m = self.maps.setdefault(sp, IMap())
            for lo, hi in runs:
                for sg in m.segs(lo, hi):
                    rsegs.append(sg)
                    if sg[0] is not None:
                        deps.append((sg[0], "RAW"))
        for ap in writes:
            sp, runs = self.regions(ap)
            m = self.maps.setdefault(sp, IMap())
            for lo, hi in runs:
                for sg in m.segs(lo, hi):
                    wsegs.append(sg)
                    if sg[0] is not None:
                        deps.append((sg[0], "WAW"))
                    for ev in sg[1].values():
                        deps.append((ev, "WAR"))
        if is_dma:
            n = self.ndma[eng]
            self.ndma[eng] += 1
            j, val = n % self.NS, 16 * (n // self.NS + 1)
            o.dma = (eng, j, val)
            if val > 16:
                deps.append((("d", eng, j, val - 16), "WAW"))
            myev = ("d", eng, j, val)
            mykey = ("d", eng, j)
        else:
            o.dma = None
            myev = ("c", eng, o.idx, o)
            mykey = eng
        waits = {}
        seen = self.seen[eng]
        for ev, kind in deps:
            if ev[0] == "c":
                src, val = ev[1], ev[2]
                if src == eng and (eng == "pe" or kind != "RAW"):
                    continue
                if seen.get(src, 0) >= val:
                    continue
                seen[src] = val
                ev[3].ms = True
                waits[src] = ev
            else:
                key = (ev[1], ev[2])
                if seen.get(key, 0) >= ev[3]:
                    continue
                seen[key] = ev[3]
                waits[key] = ev
        o.waits = list(waits.values())
        for sg in rsegs:
            sg[1][mykey] = myev
        for sg in wsegs:
            sg[0] = myev
            sg[1] = {}
        self.ops.append(o)
        return o

    def emit(self):
        rk = {e: 0 for e in self.csem}
        for o in self.ops:
            if o.ms:
                rk[o.eng] += 1
                o.rank = rk[o.eng]
        for o in self.ops:
            e = self.eng[o.eng]
            for ev in o.waits:
                if ev[0] == "c":
                    e.wait_ge(self.csem[ev[1]], ev[3].rank)
                else:
                    e.wait_ge(self.dsem[ev[1]][ev[2]], ev[3])
            ins = o.fn(e)
            if o.dma is not None:
                ins.then_inc(self.dsem[o.dma[0]][o.dma[1]], 16)
            elif o.ms:
                ins.then_inc(self.csem[o.eng], 1)

    def final_wait(self, eng, aps):
        def fn(e):
            return e.nop()
        return self.op(eng, fn, reads=aps, dma=False)


class Arena:
    def __init__(self, nc, P, nbytes):
        self.t = nc.alloc_sbuf_tensor("arena", [128, nbytes], U8)
        P.addr["arena"] = ("sb", 0)
        self.n = nbytes
        self.p = 0

    def alloc(self, shape, dt):
        n = ESZ[dt]
        for s in shape[1:]:
            n *= s
        off = (self.p + 63) // 64 * 64
        assert off + n <= self.n, ("SBUF arena overflow", off, n, self.n)
        self.p = off + n
        v = self.t[:, off:off + n].bitcast(dt)
        if len(shape) == 3:
            v = v.rearrange("p (a b) -> p a b", b=shape[2])
        elif len(shape) == 4:
            v = v.rearrange("p (a b c) -> p a b c", b=shape[2], c=shape[3])
        return v

    def ring(self, k, shape, dt):
        return [self.alloc(shape, dt) for _ in range(k)]


class Ring:
    def __init__(self, items):
        self.items = items
        self.i = 0

    def next(self):
        v = self.items[self.i % len(self.items)]
        self.i += 1
        return v


def build_nc(nlayers=DEPTH, debug=False):
    nc = bass.Bass("TRN2", target_bir_lowering=False)
    P = Prog(nc)

    def din(name, shape):
        return nc.dram_tensor(name, list(shape), F32, kind="ExternalInput").ap()

    def dscr(name, shape, dt):
        return nc.dram_tensor(name, list(shape), dt, kind=("ExternalOutput" if debug else "Internal")).ap()

    L = nlayers
    x_in = din("x", [T, D])
    xT_in = din("xT", [128, 8, T])
    pT_in = din("pT", [DEPTH, 128, 2, T])
    WF = din("WF", [DEPTH, 16, 128, 8, 128])
    WT = din("WT", [DEPTH, 3, 128, 8, 512])
    WO = din("WO", [DEPTH, 128, 8, 1024])
    WFI = din("WFI", [DEPTH, NFC, 128, 8, 256])
    WFO = din("WFO", [DEPTH, 128, NFC, 1024])
    WPG = din("WPG", [DEPTH, 128, 8, 1024])
    WPP = din("WPP", [DEPTH, 128, 2, 1024])
    CW = din("CW", [DEPTH, 128, 8, 5])
    CB = din("CB", [DEPTH, 128, 8])
    BG = din("BG", [DEPTH, 128, 16])
    NG = din("NG", [DEPTH, 128, 512])
    SK = din("SK", [DEPTH, 128, 8])
    LNP = din("LNP", [DEPTH, 4, 128, 1024])
    CID = din("CID", [128, 128])
    CMK = din("CMK", [2, 128, 128])
    CU = din("CU", [3, 128, 128])
    CBIAS = din("CBIAS", [2, 3, 128, 512])
    y_out = nc.dram_tensor("y", [T, D], F32, kind="ExternalOutput").ap()

    QKT = dscr("QKT", [8, 128, T], BF16)
    AQT = dscr("AQT", [4, 128, T], BF16)
    AKT = dscr("AKT", [4, 128, T], BF16)
    VS = dscr("VS", [T, 4, 129], BF16)
    SO = dscr("SO", [T, 512], F32)
    AVS = dscr("AVS", [T, 2, 65], BF16)
    HF = dscr("HF", [T, 512], F32)
    HS = dscr("HS", [T, 1024], BF16)
    X1 = dscr("X1", [T, D], F32)
    XB = [dscr("XB0", [T, D], F32), dscr("XB1", [T, D], F32)]
    ATS = dscr("ATS", [NT, 128, NFC, 128], BF16)

    A = Arena(nc, P, 211968)
    ps_t = nc.alloc_psum_tensor("ps", [128, 4096], F32)
    P.addr["ps"] = ("ps", 0)

    def bank(i, n=1):
        return ps_t[:, i * 512:(i + n) * 512]

    def bank_bf(i):
        return ps_t[:, i * 512:(i + 1) * 512].bitcast(BF16)

    xT = A.alloc([128, 8, T], BF16)
    ident = A.alloc([128, 128], BF16)
    mk = A.alloc([128, 2, 128], BF16)
    cu = A.alloc([128, 3, 128], F32)
    cbias = A.alloc([128, 6, 512], BF16)
    cw = A.alloc([128, 8, 5], F32)
    cb = A.alloc([128, 8], F32)
    bg = A.alloc([128, 16], F32)
    ng = A.alloc([128, 512], F32)
    sk = A.alloc([128, 8], F32)
    esk = A.alloc([128, 8], F32)
    GT = A.alloc([128, NT, 16], F32)
    LF = A.alloc([128, NT, 8], F32)
    TB = A.alloc([128, NT, 8], F32)
    TG = A.alloc([128, NT, 8], F32)
    EB = A.alloc([128, NT, 8], F32)
    WTl = A.alloc([128, NT, 8], F32)
    WHt = A.alloc([128, NT, 8], F32)
    EG = A.alloc([128, NT, 8], F32)
    phase_base = A.p

    P.op("pool", lambda e: e.dma_start(out=ident, in_=CID), writes=[ident])
    P.op("pool", lambda e: e.dma_start(out=mk, in_=CMK.rearrange("a p n -> p a n")), writes=[mk])
    P.op("sp", lambda e: e.dma_start(out=cu, in_=CU.rearrange("a p n -> p a n")), writes=[cu])
    P.op("pool", lambda e: e.dma_start(out=cbias, in_=CBIAS.rearrange("a b p n -> p (a b) n")), writes=[cbias])
    for kc in range(8):
        P.op("pool", lambda e, kc=kc: e.dma_start(out=xT[:, kc, :], in_=xT_in[:, kc, :]), writes=[xT[:, kc, :]])

    def ln_tile(y, stats, mv, rstd, gam, bet, eps):
        for h in range(2):
            P.op("dve", lambda e, h=h: e.bn_stats(out=stats[:, h, :], in_=y[:, h * 512:(h + 1) * 512]),
                 reads=[y[:, h * 512:(h + 1) * 512]], writes=[stats[:, h, :]])
        P.op("dve", lambda e: e.bn_aggr(out=mv, in_=stats), reads=[stats], writes=[mv])
        P.op("dve", lambda e: e.tensor_scalar(out=rstd, in0=mv[:, 1:2], scalar1=eps, scalar2=None, op0=ALU.add), reads=[mv], writes=[rstd])
        P.op("act", lambda e: e.activation(out=rstd, in_=rstd, func=AF.Ln), reads=[rstd], writes=[rstd])
        P.op("act", lambda e: e.activation(out=rstd, in_=rstd, func=AF.Exp, scale=-0.5), reads=[rstd], writes=[rstd])
        P.op("dve", lambda e: e.tensor_scalar(out=y, in0=y, scalar1=mv[:, 0:1], scalar2=rstd, op0=ALU.subtract, op1=ALU.mult),
             reads=[y, mv, rstd], writes=[y])
        P.op("dve", lambda e: e.tensor_tensor(out=y, in0=y, in1=gam, op=ALU.mult), reads=[y, gam], writes=[y])
        P.op("dve", lambda e: e.tensor_tensor(out=y, in0=y, in1=bet, op=ALU.add), reads=[y, bet], writes=[y])

    def to_bf(y, yb):
        P.op("act", lambda e: e.activation(out=yb, in_=y, func=AF.Copy), reads=[y], writes=[yb])

    def to_xT(yb, s, pb):
        pv = bank_bf(pb)
        for kc in range(8):
            P.op("pe", lambda e, kc=kc: e.transpose(out=pv[:, kc * 128:(kc + 1) * 128], in_=yb[:, kc * 128:(kc + 1) * 128], identity=ident),
                 reads=[yb[:, kc * 128:(kc + 1) * 128], ident], writes=[pv[:, kc * 128:(kc + 1) * 128]])
        dst = xT[:, :, s * 128:(s + 1) * 128]
        P.op("act", lambda e: e.activation(out=dst, in_=pv.rearrange("p (a b) -> p a b", b=128), func=AF.Copy),
             reads=[pv], writes=[dst])

    for l in range(L):
        Xin = x_in if l == 0 else XB[(l - 1) % 2]
        Xout = y_out if l == L - 1 else XB[l % 2]
        P.op("sp", lambda e, l=l: e.dma_start(out=cw, in_=CW[l]), writes=[cw])
        P.op("sp", lambda e, l=l: e.dma_start(out=cb, in_=CB[l]), writes=[cb])
        P.op("sp", lambda e, l=l: e.dma_start(out=bg, in_=BG[l]), writes=[bg])
        P.op("sp", lambda e, l=l: e.dma_start(out=ng, in_=NG[l]), writes=[ng])
        P.op("sp", lambda e, l=l: e.dma_start(out=sk, in_=SK[l]), writes=[sk])
        P.op("act", lambda e: e.activation(out=esk, in_=sk, func=AF.Exp), reads=[sk], writes=[esk])

        A.p = phase_base
        wf_r = Ring(A.ring(3, [128, 8, 128], BF16))
        cbuf_r = Ring(A.ring(2, [128, T + 4], F32))
        obuf_r = Ring(A.ring(2, [128, T], BF16))
        ct1_r = Ring(A.ring(4, [128, 512], F32))
        ct2_r = Ring(A.ring(2, [128, 512], F32))
        pb_r = Ring([0, 1, 2, 3])
        p1s = {}

        def p1_mm(cc):
            wf = wf_r.next()
            P.op("pool", lambda e, l=l: e.dma_start(out=wf, in_=WF[l, cc]), writes=[wf])
            is_m = cc < 8
            cbuf = None
            if is_m:
                cbuf = cbuf_r.next()
                P.op("dve", lambda e: e.memset(cbuf[:, 0:2], 0.0), writes=[cbuf[:, 0:2]])
                P.op("dve", lambda e: e.memset(cbuf[:, T + 2:T + 4], 0.0), writes=[cbuf[:, T + 2:T + 4]])
            obuf = obuf_r.next()
            for I in range(8):
                pb = bank(pb_r.next())
                for kc in range(8):
                    P.op("pe", lambda e, pb=pb, kc=kc, I=I: e.matmul(pb, lhsT=wf[:, kc, :], rhs=xT[:, kc, I * 512:(I + 1) * 512],
                                                                     start=(kc == 0), stop=(kc == 7)),
                         reads=[wf[:, kc, :], xT[:, kc, I * 512:(I + 1) * 512]], writes=[pb])
                if is_m:
                    dst = cbuf[:, 2 + I * 512:2 + (I + 1) * 512]
                    P.op("act", lambda e, dst=dst, pb=pb: e.activation(out=dst, in_=pb, func=AF.Copy), reads=[pb], writes=[dst])
                else:
                    dst = obuf[:, I * 512:(I + 1) * 512]
                    P.op("dve", lambda e, dst=dst, pb=pb: e.tensor_copy(out=dst, in_=pb), reads=[pb], writes=[dst])
            p1s[cc] = (cbuf, obuf)

        def p1_conv(cc):
            cbuf, obuf = p1s.pop(cc)
            if cc < 8:
                t1s = {}

                def tap0(I):
                    t1 = ct1_r.next()
                    s0 = cbuf[:, I * 512:I * 512 + 512]
                    P.op("act", lambda e: e.activation(out=t1, in_=s0, func=AF.Identity, bias=cb[:, cc:cc + 1], scale=cw[:, cc, 0:1]),
                         reads=[s0, cw, cb], writes=[t1])
                    t1s[I] = t1

                tap0(0)
                tap0(1)
                for I in range(8):
                    if I % 2 == 0:
                        for k in range(1, 5):
                            for I2 in (I, I + 1):
                                sk_ = cbuf[:, I2 * 512 + k:I2 * 512 + k + 512]
                                P.op("dve", lambda e, t1=t1s[I2], sk_=sk_, k=k: e.scalar_tensor_tensor(out=t1, in0=sk_, scalar=cw[:, cc, k:k + 1], in1=t1,
                                                                                                       op0=ALU.mult, op1=ALU.add),
                                     reads=[sk_, cw, t1s[I2]], writes=[t1s[I2]])
                        if I + 2 < 8:
                            tap0(I + 2)
                            tap0(I + 3)
                    t1 = t1s.pop(I)
                    t2 = ct2_r.next()
                    dst = obuf[:, I * 512:(I + 1) * 512]
                    if cc < 4:
                        P.op("act", lambda e, dst=dst, t1=t1: e.activation(out=dst, in_=t1, func=AF.Silu), reads=[t1], writes=[dst])
                    else:
                        P.op("act", lambda e, t2=t2, t1=t1: e.activation(out=t2, in_=t1, func=AF.Silu), reads=[t1], writes=[t2])
                        P.op("dve", lambda e, dst=dst, t2=t2: e.tensor_scalar(out=dst, in0=t2, scalar1=float(128 ** -0.5), scalar2=None, op0=ALU.mult),
                             reads=[t2], writes=[dst])
                dd = QKT[cc]
            elif cc < 12:
                dd = AQT[cc - 8]
            else:
                dd = AKT[cc - 12]
            P.op("sp", lambda e: e.dma_start(out=dd, in_=obuf), reads=[obuf], writes=[dd])

        p1_mm(0)
        for cc in range(16):
            if cc + 1 < 16:
                p1_mm(cc + 1)
            p1_conv(cc)

        A.p = phase_base
        wt_r = Ring(A.ring(2, [128, 8, 512], BF16))
        vst_r = Ring(A.ring(3, [128, 4, 129], BF16))
        sot_r = Ring(A.ring(3, [128, 512], F32))
        avt_r = Ring(A.ring(3, [128, 2, 65], BF16))
        for v in vst_r.items:
            P.op("dve", lambda e, v=v: e.memset(v[:, :, 128:129], 1.0), writes=[v[:, :, 128:129]])
        for v in avt_r.items:
            P.op("dve", lambda e, v=v: e.memset(v[:, :, 64:65], 1.0), writes=[v[:, :, 64:65]])
        pb_r = Ring([0, 1, 2, 3])
        for blk in range(3):
            wt = wt_r.next()
            P.op("pool", lambda e, wt=wt, l=l, blk=blk: e.dma_start(out=wt, in_=WT[l, blk]), writes=[wt])
            ncol = 512 if blk < 2 else 144
            for s in range(NT):
                pb = bank(pb_r.next())[:, 0:ncol]
                for kc in range(8):
                    P.op("pe", lambda e, pb=pb, wt=wt, kc=kc, s=s, ncol=ncol: e.matmul(pb, lhsT=xT[:, kc, s * 128:(s + 1) * 128], rhs=wt[:, kc, 0:ncol],
                                                                                        start=(kc == 0), stop=(kc == 7)),
                         reads=[wt[:, kc, 0:ncol], xT[:, kc, s * 128:(s + 1) * 128]], writes=[pb])
                if blk == 0:
                    st = vst_r.next()
                    P.op("act", lambda e, st=st, pb=pb: e.activation(out=st[:, :, 0:128], in_=pb.rearrange("p (a b) -> p a b", b=128), func=AF.Copy),
                         reads=[pb], writes=[st[:, :, 0:128]])
                    P.op("sp", lambda e, st=st, s=s: e.dma_start(out=VS[s * 128:(s + 1) * 128], in_=st), reads=[st], writes=[VS[s * 128:(s + 1) * 128]])
                elif blk == 1:
                    st = sot_r.next()
                    P.op("act", lambda e, st=st, pb=pb: e.activation(out=st, in_=pb, func=AF.Sigmoid), reads=[pb], writes=[st])
                    P.op("dve", lambda e, st=st: e.tensor_tensor(out=st, in0=st, in1=ng, op=ALU.mult), reads=[st, ng], writes=[st])
                    P.op("sp", lambda e, st=st, s=s: e.dma_start(out=SO[s * 128:(s + 1) * 128], in_=st), reads=[st], writes=[SO[s * 128:(s + 1) * 128]])
                else:
                    P.op("dve", lambda e, pb=pb, s=s: e.tensor_tensor(out=GT[:, s, :], in0=pb[:, 0:16], in1=bg, op=ALU.add),
                         reads=[pb, bg], writes=[GT[:, s, :]])
                    st = avt_r.next()
                    P.op("act", lambda e, st=st, pb=pb: e.activation(out=st[:, :, 0:64], in_=pb[:, 16:144].rearrange("p (a b) -> p a b", b=64), func=AF.Copy),
                         reads=[pb], writes=[st[:, :, 0:64]])
                    P.op("sp", lambda e, st=st, s=s: e.dma_start(out=AVS[s * 128:(s + 1) * 128], in_=st), reads=[st], writes=[AVS[s * 128:(s + 1) * 128]])

        A.p = phase_base
        tmpg = A.alloc([128, NT, 8], F32)
        P.op("act", lambda e: e.activation(out=tmpg, in_=GT[:, :, 8:16], func=AF.Exp, scale=-1.0), reads=[GT], writes=[tmpg])
        P.op("act", lambda e: e.activation(out=tmpg, in_=tmpg, func=AF.Ln, bias=1.0), reads=[tmpg], writes=[tmpg])
        P.op("dve", lambda e: e.tensor_scalar(out=LF, in0=tmpg, scalar1=-1.0, scalar2=None, op0=ALU.mult), reads=[tmpg], writes=[LF])
        pB = bank(4)[:, 0:NT * 8].rearrange("p (a b) -> p a b", b=8)
        pG = bank(5)[:, 0:NT * 8].rearrange("p (a b) -> p a b", b=8)
        for s in range(NT):
            for d_ in range(2):
                P.op("pe", lambda e, s=s, d_=d_: e.matmul(pB[:, s, d_ * 4:(d_ + 1) * 4], lhsT=cu[:, d_, :], rhs=LF[:, s, d_ * 4:(d_ + 1) * 4], start=True, stop=True),
                     reads=[cu, LF[:, s, :]], writes=[pB[:, s, d_ * 4:(d_ + 1) * 4]])
            P.op("pe", lambda e, s=s: e.matmul(pG[:, s, :], lhsT=cu[:, 2, :], rhs=LF[:, s, :], start=True, stop=True),
                 reads=[cu, LF[:, s, :]], writes=[pG[:, s, :]])
        P.op("dve", lambda e: e.tensor_copy(out=TB, in_=pB), reads=[pB], writes=[TB])
        P.op("dve", lambda e: e.tensor_copy(out=TG, in_=pG), reads=[pG], writes=[TG])
        P.op("act", lambda e: e.activation(out=EB, in_=TB, func=AF.Exp), reads=[TB], writes=[EB])
        P.op("act", lambda e: e.activation(out=EG, in_=TG, func=AF.Exp), reads=[TG], writes=[EG])
        EBI = LF
        P.op("act", lambda e: e.activation(out=EBI, in_=TB, func=AF.Exp, scale=-1.0), reads=[TB], writes=[EBI])
        P.op("dve", lambda e: e.tensor_tensor(out=tmpg, in0=GT[:, :, 0:8], in1=TB, op=ALU.subtract), reads=[GT, TB], writes=[tmpg])
        P.op("act", lambda e: e.activation(out=WTl, in_=tmpg, func=AF.Exp), reads=[tmpg], writes=[WTl])
        P.op("dve", lambda e: e.tensor_tensor(out=tmpg, in0=tmpg, in1=TG, op=ALU.add), reads=[tmpg, TG, WTl], writes=[tmpg])
        P.op("act", lambda e: e.activation(out=WHt, in_=tmpg, func=AF.Exp), reads=[tmpg], writes=[WHt])

        A.p = phase_base
        wo = A.alloc([128, 8, 1024], BF16)
        lng = A.alloc([128, 1024], F32)
        lnb = A.alloc([128, 1024], F32)
        for kc in range(0, 8, 4):
            P.op("pool", lambda e, l=l, kc=kc: e.dma_start(out=wo[:, kc:kc + 4, :], in_=WO[l, :, kc:kc + 4, :]), writes=[wo[:, kc:kc + 4, :]])
        P.op("sp", lambda e, l=l: e.dma_start(out=lng, in_=LNP[l, 0]), writes=[lng])
        P.op("sp", lambda e, l=l: e.dma_start(out=lnb, in_=LNP[l, 1]), writes=[lnb])
        p6_base = A.p
        aq_r = Ring(A.ring(2, [128, 4, 128], BF16))
        ak_r = A.ring(4, [128, 4, 128], BF16)
        av_r = A.ring(4, [128, 2, 65], BF16)
        pt_r = Ring(A.ring(5, [128, 512], BF16))
        ha_r = Ring(A.ring(2, [128, 8, 64], BF16))
        dn_r = Ring(A.ring(2, [128, 8], F32))
        ps_r = Ring([6])
        po_r = Ring([7])

        def load_kv(j):
            ak, av = ak_r[j % 4], av_r[j % 4]
            P.op("sp", lambda e: e.dma_start(out=ak, in_=AKT[:, :, j * 128:(j + 1) * 128].rearrange("c p t -> p c t")),
                 reads=[AKT[:, :, j * 128:(j + 1) * 128]], writes=[ak])
            P.op("sp", lambda e: e.dma_start(out=av, in_=AVS[j * 128:(j + 1) * 128]), reads=[AVS[j * 128:(j + 1) * 128]], writes=[av])

        aqs = {}

        def att_loads(s):
            if s + 1 < NT:
                load_kv(s + 1)
            aq = aq_r.next()
            P.op("sp", lambda e, aq=aq, s=s: e.dma_start(out=aq, in_=AQT[:, :, s * 128:(s + 1) * 128].rearrange("c p t -> p c t")),
                 reads=[AQT[:, :, s * 128:(s + 1) * 128]], writes=[aq])
            aqs[s] = aq

        load_kv(0)
        att_loads(0)

        def att_block(s):
            if s + 1 < NT:
                att_loads(s + 1)
            aq = aqs.pop(s)
            ha = ha_r.next()
            dn = dn_r.next()
            kbs = [j for j in (s - 1, s, s + 1) if 0 <= j < NT]
            for kvh in range(2):
                po = bank(po_r.next()).rearrange("p (a b) -> p a b", b=128)
                pts = []
                for j in kbs:
                    rel = j - s + 1
                    ak = ak_r[j % 4]
                    pb = bank(ps_r.next())
                    bt = cbias[:, kvh * 3 + rel, :]
                    P.op("pe", lambda e, pb=pb, bt=bt: e.matmul(pb, lhsT=ident, rhs=bt, start=True, stop=False), reads=[ident, bt], writes=[pb])
                    for g in range(4):
                        hd = 4 * kvh + g
                        P.op("pe", lambda e, pb=pb, ak=ak, aq=aq, g=g, hd=hd, kvh=kvh: e.matmul(pb[:, g * 128:(g + 1) * 128], lhsT=ak[:, kvh * 2 + (hd % 2), :],
                                                                                                  rhs=aq[:, hd // 2, :], start=False, stop=(g == 3)),
                             reads=[ak[:, kvh * 2 + (hd % 2), :], aq[:, hd // 2, :]], writes=[pb[:, g * 128:(g + 1) * 128]])
                    pt = pt_r.next()
                    P.op("act", lambda e, pt=pt, pb=pb: e.activation(out=pt, in_=pb, func=AF.Exp, scale=0.125), reads=[pb], writes=[pt])
                    pts.append((pt, j))
                    yield
                for g in range(4):
                    for n_, (pt, j) in enumerate(pts):
                        av = av_r[j % 4]
                        P.op("pe", lambda e, po=po, pt=pt, av=av, g=g, n_=n_, kvh=kvh, last=(n_ == len(pts) - 1): e.matmul(
                            po[:, g, 0:65], lhsT=pt[:, g * 128:(g + 1) * 128], rhs=av[:, kvh, :], start=(n_ == 0), stop=last),
                            reads=[pt[:, g * 128:(g + 1) * 128], av[:, kvh, :]], writes=[po[:, g, 0:65]])
                dk = dn[:, kvh * 4:(kvh + 1) * 4]
                P.op("dve", lambda e, dk=dk, po=po, kvh=kvh: e.tensor_tensor(out=dk, in0=po[:, :, 64], in1=esk[:, kvh * 4:(kvh + 1) * 4], op=ALU.add),
                     reads=[po, esk], writes=[dk])
                P.op("dve", lambda e, dk=dk: e.reciprocal(out=dk, in_=dk), reads=[dk], writes=[dk])
                hk = ha[:, kvh * 4:(kvh + 1) * 4, :]
                P.op("dve", lambda e, hk=hk, po=po, dk=dk: e.tensor_tensor(out=hk, in0=po[:, :, 0:64], in1=dk.unsqueeze(2).to_broadcast([128, 4, 64]), op=ALU.mult),
                     reads=[po, dk], writes=[hk])
            P.op("sp", lambda e, ha=ha, s=s: e.dma_start(out=HS[s * 128:(s + 1) * 128, 512:1024], in_=ha.rearrange("p a b -> p (a b)")),
                 reads=[ha], writes=[HS[s * 128:(s + 1) * 128, 512:1024]])

        qt_r = A.ring(4, [128, 4, 128], BF16)
        kt_r = A.ring(4, [128, 4, 128], BF16)
        vj_r = A.ring(4, [128, 4, 129], BF16)
        vt_r = A.ring(2, [128, 4, 129], BF16)
        ktk_r = A.ring(2, [128, 4, 128], BF16)
        at_r = A.ring(2, [128, 4, 128], BF16)
        Cn = A.alloc([128, 4, 129], F32)
        Cb3 = A.ring(3, [128, 4, 129], BF16)
        hd_r = A.ring(3, [128, 4, 128], F32)
        hf_r = A.ring(5, [128, 4, 128], F32)
        so_r = A.ring(5, [128, 4, 128], F32)
        sm_r = A.ring(3, [128, 32], F32)
        sq_r = A.ring(3, [128, 4, 128], F32)
        hb_r = A.ring(2, [128, 512], BF16)
        NG2 = 2 * NT

        def step_of(g):
            dr, n_ = g // NT, g % NT
            return dr, n_, (n_ if dr == 0 else NT - 1 - n_)

        def scan_loads2(g):
            dr, n_, j = step_of(g)
            tsl = slice(j * 128, (j + 1) * 128)
            hft, sot = hf_r[g % 5], so_r[g % 5]
            P.op("sp", lambda e: e.dma_start(out=hft.rearrange("p a b -> p (a b)"), in_=HF[tsl]), reads=[HF[tsl]], writes=[hft])
            P.op("sp", lambda e: e.dma_start(out=sot.rearrange("p a b -> p (a b)"), in_=SO[tsl]), reads=[SO[tsl]], writes=[sot])

        def scan_loads(g):
            dr, n_, j = step_of(g)
            tsl = slice(j * 128, (j + 1) * 128)
            qt, kt, vj = qt_r[g % 4], kt_r[g % 4], vj_r[g % 4]
            P.op("sp", lambda e: e.dma_start(out=qt, in_=QKT[0:4, :, tsl].rearrange("c p t -> p c t")), reads=[QKT[0:4, :, tsl]], writes=[qt])
            P.op("sp", lambda e: e.dma_start(out=kt, in_=QKT[4:8, :, tsl].rearrange("c p t -> p c t")), reads=[QKT[4:8, :, tsl]], writes=[kt])
            P.op("sp", lambda e: e.dma_start(out=vj, in_=VS[tsl]), reads=[VS[tsl]], writes=[vj])
            if dr == 1 and n_ >= 3:
                scan_loads2(g)

        def sx(g):
            dr, n_, j = step_of(g)
            qt, kt, vj = qt_r[g % 4], kt_r[g % 4], vj_r[g % 4]
            vt, ktk, at = vt_r[g % 2], ktk_r[g % 2], at_r[g % 2]
            cb_wr = Cb3[(g + 1) % 3]
            if n_ == 0:
                P.op("dve", lambda e: e.memset(Cn, 0.0), writes=[Cn])
                P.op("dve", lambda e: e.memset(Cb3[g % 3], 0.0), writes=[Cb3[g % 3]])
            wts = WTl[:, j, dr * 4:(dr + 1) * 4]
            egs = EG[:, j, dr * 4:(dr + 1) * 4]
            P.op("dve", lambda e: e.tensor_tensor(out=vt, in0=vj, in1=wts.unsqueeze(2).to_broadcast([128, 4, 129]), op=ALU.mult),
                 reads=[vj, wts], writes=[vt])
            pk = bank_bf(0)
            for h in range(4):
                P.op("pe", lambda e, h=h: e.transpose(out=pk[:, h * 128:(h + 1) * 128], in_=kt[:, h, :], identity=ident),
                     reads=[kt[:, h, :], ident], writes=[pk[:, h * 128:(h + 1) * 128]])
            P.op("dve", lambda e: e.tensor_copy(out=ktk, in_=pk[:, 0:512].rearrange("p (a b) -> p a b", b=128)),
                 reads=[pk[:, 0:512]], writes=[ktk])
            pS = bank(1)
            for h in range(4):
                P.op("pe", lambda e, h=h: e.matmul(pS[:, h * 128:(h + 1) * 128], lhsT=kt[:, h, :], rhs=qt[:, h, :], start=True, stop=True),
                     reads=[kt[:, h, :], qt[:, h, :]], writes=[pS[:, h * 128:(h + 1) * 128]])
            pC = bank(4, 2).rearrange("p (a b) -> p a b", b=256)
            for h in range(4):
                P.op("pe", lambda e, h=h: e.matmul(pC[:, h, 0:129], lhsT=ktk[:, h, :], rhs=vt[:, h, :], start=True, stop=True),
                     reads=[ktk[:, h, :], vt[:, h, :]], writes=[pC[:, h, 0:129]])
            P.op("dve", lambda e: e.tensor_tensor(out=Cn, in0=Cn, in1=pC[:, :, 0:129], op=ALU.add), reads=[Cn, pC], writes=[Cn])
            P.op("dve", lambda e: e.tensor_tensor(out=Cn, in0=Cn, in1=egs.unsqueeze(2).to_broadcast([128, 4, 129]), op=ALU.mult),
                 reads=[Cn, egs], writes=[Cn])
            P.op("act", lambda e: e.activation(out=cb_wr, in_=Cn, func=AF.Copy), reads=[Cn], writes=[cb_wr])
            P.op("dve", lambda e: e.tensor_tensor(out=at, in0=pS.rearrange("p (a b) -> p a b", b=128),
                                                  in1=mk[:, dr, :].unsqueeze(1).to_broadcast([128, 4, 128]), op=ALU.mult),
                 reads=[pS, mk], writes=[at])

        def sy(g):
            dr, n_, j = step_of(g)
            tsl = slice(j * 128, (j + 1) * 128)
            qt = qt_r[g % 4]
            vt, at = vt_r[g % 2], at_r[g % 2]
            cb_rd = Cb3[g % 3]
            ebi = EBI[:, j, dr * 4:(dr + 1) * 4]
            pN = bank(2, 2).rearrange("p (a b) -> p a b", b=256)
            for h in range(4):
                P.op("pe", lambda e, h=h: e.matmul(pN[:, h, 0:129], lhsT=at[:, h, :], rhs=vt[:, h, :], start=True, stop=False),
                     reads=[at[:, h, :], vt[:, h, :]], writes=[pN[:, h, 0:129]])
                P.op("pe", lambda e, h=h: e.matmul(pN[:, h, 0:129], lhsT=qt[:, h, :], rhs=cb_rd[:, h, :], start=False, stop=True),
                     reads=[qt[:, h, :], cb_rd[:, h, :]], writes=[pN[:, h, 0:129]])
            sm = sm_r[g % 3]
            t1, t2, rr = sm[:, 0:4], sm[:, 4:8], sm[:, 8:12]
            P.op("dve", lambda e: e.tensor_tensor(out=t1, in0=pN[:, :, 128], in1=ebi, op=ALU.max), reads=[pN, ebi], writes=[t1])
            P.op("dve", lambda e: e.scalar_tensor_tensor(out=t2, in0=pN[:, :, 128], scalar=-1.0, in1=t1, op0=ALU.mult, op1=ALU.max),
                 reads=[pN, t1], writes=[t2])
            P.op("dve", lambda e: e.reciprocal(out=rr, in_=t2), reads=[t2], writes=[rr])
            hdt = hd_r[g % 3]
            P.op("dve", lambda e: e.tensor_tensor(out=hdt, in0=pN[:, :, 0:128], in1=rr.unsqueeze(2).to_broadcast([128, 4, 128]), op=ALU.mult),
                 reads=[pN, rr], writes=[hdt])
            if dr == 0:
                P.op("sp", lambda e: e.dma_start(out=HF[tsl], in_=hdt.rearrange("p a b -> p (a b)")), reads=[hdt], writes=[HF[tsl]])
                return
            hft = hf_r[g % 5]
            P.op("dve", lambda e: e.tensor_tensor(out=hdt, in0=hdt, in1=hft, op=ALU.add), reads=[hdt, hft], writes=[hdt])

        def sy2(g):
            dr, n_, j = step_of(g)
            if dr == 0:
                return
            sm = sm_r[g % 3]
            hdt = hd_r[g % 3]
            ssum, ssq, mu, m2, var, rs = sm[:, 12:16], sm[:, 16:20], sm[:, 20:24], sm[:, 24:28], sm[:, 28:32], sm[:, 0:4]
            sq = sq_r[g % 3]
            P.op("dve", lambda e: e.memset(ssq, 0.0), writes=[ssq])
            P.op("dve", lambda e: e.tensor_reduce(out=ssum, in_=hdt, axis=AX.X, op=ALU.add), reads=[hdt], writes=[ssum])
            for h in range(4):
                P.op("act", lambda e, h=h: e.activation(out=sq[:, h, :], in_=hdt[:, h, :], func=AF.Square, accum_out=ssq[:, h:h + 1]),
                     reads=[hdt[:, h, :], ssq[:, h:h + 1]], writes=[sq[:, h, :], ssq[:, h:h + 1]])
            P.op("dve", lambda e: e.tensor_scalar(out=mu, in0=ssum, scalar1=1.0 / 128, scalar2=None, op0=ALU.mult), reads=[ssum], writes=[mu])
            P.op("dve", lambda e: e.tensor_tensor(out=m2, in0=mu, in1=mu, op=ALU.mult), reads=[mu], writes=[m2])
            P.op("dve", lambda e: e.tensor_scalar(out=var, in0=ssq, scalar1=1.0 / 128, scalar2=MH_EPS, op0=ALU.mult, op1=ALU.add),
                 reads=[ssq], writes=[var])
            P.op("dve", lambda e: e.tensor_tensor(out=var, in0=var, in1=m2, op=ALU.subtract), reads=[var, m2], writes=[var])
            P.op("act", lambda e: e.activation(out=rs, in_=var, func=AF.Ln), reads=[var], writes=[rs])
            P.op("act", lambda e: e.activation(out=rs, in_=rs, func=AF.Exp, scale=-0.5), reads=[rs], writes=[rs])

        def sz(g):
            dr, n_, j = step_of(g)
            if dr == 0:
                return
            tsl = slice(j * 128, (j + 1) * 128)
            sm = sm_r[g % 3]
            mu, rs, nb = sm[:, 20:24], sm[:, 0:4], sm[:, 4:8]
            hdt, sq, sot, hb = hd_r[g % 3], sq_r[g % 3], so_r[g % 5], hb_r[g % 2]
            P.op("dve", lambda e: e.scalar_tensor_tensor(out=nb, in0=mu, scalar=-1.0, in1=rs, op0=ALU.mult, op1=ALU.mult),
                 reads=[mu, rs], writes=[nb])
            for h in range(4):
                P.op("act", lambda e, h=h: e.activation(out=sq[:, h, :], in_=hdt[:, h, :], func=AF.Identity,
                                                        bias=nb[:, h:h + 1], scale=rs[:, h:h + 1]),
                     reads=[hdt[:, h, :], rs, nb], writes=[sq[:, h, :]])
            P.op("dve", lambda e: e.tensor_tensor(out=hb.rearrange("p (a b) -> p a b", b=128), in0=sq, in1=sot, op=ALU.mult),
                 reads=[sq, sot], writes=[hb])
            P.op("sp", lambda e: e.dma_start(out=HS[tsl, 0:512], in_=hb), reads=[hb], writes=[HS[tsl, 0:512]])

        scan_loads(0)
        scan_loads(1)
        scan_loads(2)
        sx(0)
        def att_units():
            for s_ in range(NT):
                yield from att_block(s_)

        units = att_units()

        def att_step():
            try:
                next(units)
            except StopIteration:
                pass

        for g in range(NG2):
            dr, n_, j = step_of(g)
            if dr == 1 and n_ < 3:
                scan_loads2(g)
            att_step()
            if g + 1 < NG2:
                sx(g + 1)
            att_step()
            ca = P.capture(sy, g)
            cbq = P.capture(sy2, g - 1) if g >= 1 else []
            a_pe = [o_ for o_ in ca if o_[0] == "pe"]
            a_dve = [o_ for o_ in ca if o_[0] == "dve"]
            a_oth = [o_ for o_ in ca if o_[0] not in ("pe", "dve")]
            b_dve = [o_ for o_ in cbq if o_[0] == "dve"]
            b_act = [o_ for o_ in cbq if o_[0] == "act"]
            for o_ in a_pe:
                P.op(*o_)
            if b_dve:
                P.op(*b_dve.pop(0))
                for o_ in b_act[:4]:
                    P.op(*o_)
                b_act = b_act[4:]
            while a_dve or b_dve:
                if a_dve:
                    P.op(*a_dve.pop(0))
                if b_dve:
                    P.op(*b_dve.pop(0))
            for o_ in a_oth + b_act:
                P.op(*o_)
            att_step()
            if g >= 2:
                sz(g - 2)
            if g + 3 < NG2:
                scan_loads(g + 3)
        sy2(NG2 - 1)
        sz(NG2 - 2)
        sz(NG2 - 1)
        for _ in units:
            pass

        A.p = p6_base
        hin_r = Ring(A.ring(3, [128, 1024], BF16))
        xin_r = Ring(A.ring(3, [128, 1024], F32))
        hT_r = Ring(A.ring(2, [128, 8, 128], BF16))
        yb_r = Ring(A.ring(2, [128, 1024], BF16))
        st_r = Ring(A.ring(2, [128, 2, 6], F32))
        mv_r = Ring(A.ring(2, [128, 4], F32))
        st6 = {}

        ld6 = {}

        def p6_L(s):
            tsl = slice(s * 128, (s + 1) * 128)
            hin, xin = hin_r.next(), xin_r.next()
            P.op("sp", lambda e: e.dma_start(out=hin, in_=HS[tsl]), reads=[HS[tsl]], writes=[hin])
            P.op("sp", lambda e, Xin=Xin: e.dma_start(out=xin, in_=Xin[tsl]), reads=[Xin[tsl]], writes=[xin])
            ld6[s] = (hin, xin)

        def p6_A(s):
            tsl = slice(s * 128, (s + 1) * 128)
            hin, xin = ld6.pop(s)
            pv = bank_bf(6 + (s % 2))
            for kc in range(8):
                P.op("pe", lambda e, kc=kc: e.transpose(out=pv[:, kc * 128:(kc + 1) * 128], in_=hin[:, kc * 128:(kc + 1) * 128], identity=ident),
                     reads=[hin[:, kc * 128:(kc + 1) * 128], ident], writes=[pv[:, kc * 128:(kc + 1) * 128]])
            hT = hT_r.next()
            P.op("act", lambda e: e.activation(out=hT, in_=pv.rearrange("p (a b) -> p a b", b=128), func=AF.Copy), reads=[pv], writes=[hT])
            pm = bank(2 * (s % 2), 2)
            for n_ in range(2):
                for kc in range(8):
                    P.op("pe", lambda e, kc=kc, n_=n_: e.matmul(pm[:, n_ * 512:(n_ + 1) * 512], lhsT=hT[:, kc, :], rhs=wo[:, kc, n_ * 512:(n_ + 1) * 512],
                                                                start=(kc == 0), stop=(kc == 7)),
                         reads=[hT[:, kc, :], wo[:, kc, n_ * 512:(n_ + 1) * 512]], writes=[pm[:, n_ * 512:(n_ + 1) * 512]])
            st6[s] = (xin, pm, tsl)

        def p6_B(s):
            xin, pm, tsl = st6[s]
            P.op("dve", lambda e: e.scalar_tensor_tensor(out=xin, in0=xin, scalar=float(ALPHA), in1=pm, op0=ALU.mult, op1=ALU.add),
                 reads=[xin, pm], writes=[xin])
            mv = mv_r.next()
            ln_tile(xin, st_r.next(), mv[:, 0:2], mv[:, 2:3], lng, lnb, LN_EPS)
            P.op("sp", lambda e: e.dma_start(out=X1[tsl], in_=xin), reads=[xin], writes=[X1[tsl]])
            yb = yb_r.next()
            to_bf(xin, yb)
            st6[s] = yb

        p6_L(0)
        p6_L(1)
        p6_A(0)
        for s in range(NT):
            if s + 2 < NT:
                p6_L(s + 2)
            if s + 1 < NT:
                p6_A(s + 1)
            p6_B(s)
            to_xT(st6[s], s, 4 + (s % 2))

        A.p = phase_base
        wfi_r = Ring(A.ring(3, [128, 8, 256], BF16))
        sl_r = Ring(A.ring(3, [128, 512], F32))
        ao_r = Ring(A.ring(3, [128, 512], BF16))
        pb_r = Ring([0, 1, 2, 3, 4, 5, 6, 7])
        P7B_SMALL = 40 * 1024
        assert A.p <= phase_base + P7B_SMALL
        A.p = phase_base + P7B_SMALL
        wpg = A.alloc([128, 8, 1024], BF16)
        wpp = A.alloc([128, 2, 1024], BF16)
        pT = A.alloc([128, 2, T], BF16)
        wfo = A.alloc([128, NFC, 1024], BF16)
        w7 = []
        for kc in range(0, 8, 4):
            w7.append(lambda l=l, kc=kc: P.op("pool", lambda e: e.dma_start(out=wpg[:, kc:kc + 4, :], in_=WPG[l, :, kc:kc + 4, :]), writes=[wpg[:, kc:kc + 4, :]]))
        w7.append(lambda l=l: P.op("pool", lambda e: e.dma_start(out=wpp, in_=WPP[l]), writes=[wpp]))
        for kc in range(2):
            w7.append(lambda l=l, kc=kc: P.op("pool", lambda e: e.dma_start(out=pT[:, kc, :], in_=pT_in[l, :, kc, :]), writes=[pT[:, kc, :]]))
        for c0 in range(0, NFC, 4):
            c1 = min(NFC, c0 + 4)
            w7.append(lambda l=l, c0=c0, c1=c1: P.op("pool", lambda e: e.dma_start(out=wfo[:, c0:c1, :], in_=WFO[l, :, c0:c1, :]), writes=[wfo[:, c0:c1, :]]))
        for c in range(NFC):
            if c >= 3 and w7:
                w7.pop(0)()
            wfi = wfi_r.next()
            P.op("pool", lambda e, wfi=wfi, l=l, c=c: e.dma_start(out=wfi, in_=WFI[l, c]), writes=[wfi])
            for I in range(8):
                pg, pu = bank(pb_r.next()), bank(pb_r.next())
                for half, pb in ((0, pg), (1, pu)):
                    for kc in range(8):
                        P.op("pe", lambda e, pb=pb, wfi=wfi, kc=kc, I=I, half=half: e.matmul(pb, lhsT=wfi[:, kc, half * 128:(half + 1) * 128],
                                                                                              rhs=xT[:, kc, I * 512:(I + 1) * 512], start=(kc == 0), stop=(kc == 7)),
                             reads=[wfi[:, kc, half * 128:(half + 1) * 128], xT[:, kc, I * 512:(I + 1) * 512]], writes=[pb])
                sl, ao = sl_r.next(), ao_r.next()
                P.op("act", lambda e, sl=sl, pg=pg: e.activation(out=sl, in_=pg, func=AF.Silu), reads=[pg], writes=[sl])
                P.op("dve", lambda e, ao=ao, sl=sl, pu=pu: e.tensor_tensor(out=ao, in0=sl, in1=pu, op=ALU.mult), reads=[sl, pu], writes=[ao])
                dst = ATS[4 * I:4 * I + 4, :, c, :].rearrange("s p j -> p s j")
                P.op("sp", lambda e, dst=dst, ao=ao: e.dma_start(out=dst, in_=ao.rearrange("p (s j) -> p s j", j=128)), reads=[ao], writes=[dst])

        while w7:
            w7.pop(0)()

        A.p = phase_base
        lng2 = A.alloc([128, 1024], F32)
        lnb2 = A.alloc([128, 1024], F32)
        ai_r = Ring(A.ring(2, [128, NFC, 128], BF16))
        x1_r = Ring(A.ring(2, [128, 1024], F32))
        sg_r = Ring(A.ring(2, [128, 1024], F32))
        yb_r = Ring(A.ring(2, [128, 1024], BF16))
        st_r = Ring(A.ring(2, [128, 2, 6], F32))
        mv_r = Ring(A.ring(2, [128, 4], F32))
        assert A.p <= phase_base + P7B_SMALL, (A.p - phase_base)
        P.op("sp", lambda e, l=l: e.dma_start(out=lng2, in_=LNP[l, 2]), writes=[lng2])
        P.op("sp", lambda e, l=l: e.dma_start(out=lnb2, in_=LNP[l, 3]), writes=[lnb2])
        st7 = {}

        ld7a, ld7x = {}, {}

        def p7_La(s):
            ai = ai_r.next()
            P.op("sp", lambda e: e.dma_start(out=ai, in_=ATS[s]), reads=[ATS[s]], writes=[ai])
            ld7a[s] = ai

        def p7_Lx(s):
            tsl = slice(s * 128, (s + 1) * 128)
            x1 = x1_r.next()
            P.op("sp", lambda e: e.dma_start(out=x1, in_=X1[tsl]), reads=[X1[tsl]], writes=[x1])
            ld7x[s] = x1

        def p7_A(s):
            tsl = slice(s * 128, (s + 1) * 128)
            ai, x1 = ld7a.pop(s), ld7x.pop(s)
            pgp = bank(0, 2)
            ppp = bank(2, 2)
            pff = bank(4, 2)
            for n_ in range(2):
                cs = slice(n_ * 512, (n_ + 1) * 512)
                for kc in range(8):
                    P.op("pe", lambda e, kc=kc, cs=cs: e.matmul(pgp[:, cs], lhsT=xT[:, kc, tsl], rhs=wpg[:, kc, cs], start=(kc == 0), stop=(kc == 7)),
                         reads=[xT[:, kc, tsl], wpg[:, kc, cs]], writes=[pgp[:, cs]])
                for kc in range(2):
                    P.op("pe", lambda e, kc=kc, cs=cs: e.matmul(ppp[:, cs], lhsT=pT[:, kc, tsl], rhs=wpp[:, kc, cs], start=(kc == 0), stop=(kc == 1)),
                         reads=[pT[:, kc, tsl], wpp[:, kc, cs]], writes=[ppp[:, cs]])
            sg = sg_r.next()
            P.op("act", lambda e: e.activation(out=sg, in_=pgp, func=AF.Sigmoid), reads=[pgp], writes=[sg])
            st7[s] = (x1, sg, pff, tsl, ai)
            st7[("ppp", s)] = ppp

        def p7_A2(s):
            x1, sg, pff, tsl, ai = st7[s]

            for n_ in range(2):
                cs = slice(n_ * 512, (n_ + 1) * 512)
                for c in range(NFC):
                    P.op("pe", lambda e, c=c, cs=cs: e.matmul(pff[:, cs], lhsT=ai[:, c, :], rhs=wfo[:, c, cs], start=(c == 0), stop=(c == NFC - 1)),
                         reads=[ai[:, c, :], wfo[:, c, cs]], writes=[pff[:, cs]])

        def p7_M(s):
            x1, sg, pff, tsl, ai = st7[s]
            ppp = st7.pop(("ppp", s))
            P.op("dve", lambda e: e.tensor_tensor(out=sg, in0=sg, in1=ppp, op=ALU.mult), reads=[sg, ppp], writes=[sg])

        def p7_Bx(s):
            x1, sg, pff, tsl, ai = st7[s]
            P.op("dve", lambda e: e.scalar_tensor_tensor(out=x1, in0=x1, scalar=float(ALPHA), in1=pff, op0=ALU.mult, op1=ALU.add),
                 reads=[x1, pff], writes=[x1])

        def p7_B(s):
            x1, sg, pff, tsl, ai = st7[s]
            P.op("dve", lambda e: e.tensor_tensor(out=x1, in0=x1, in1=sg, op=ALU.add), reads=[x1, sg], writes=[x1])
            mv = mv_r.next()
            ln_tile(x1, st_r.next(), mv[:, 0:2], mv[:, 2:3], lng2, lnb2, LN_EPS)
            P.op("sp", lambda e, Xout=Xout: e.dma_start(out=Xout[tsl], in_=x1), reads=[x1], writes=[Xout[tsl]])
            if l < L - 1:
                yb = yb_r.next()
                to_bf(x1, yb)
                st7[s] = yb

        p7_La(0)
        p7_La(1)
        p7_Lx(0)
        p7_A(0)
        p7_M(0)
        p7_A2(0)
        for s in range(NT):
            if s + 2 < NT:
                p7_La(s + 2)
            if s + 1 < NT:
                p7_Lx(s + 1)
            if s + 1 < NT:
                p7_A(s + 1)
            p7_Bx(s)
            if s + 1 < NT:
                p7_A2(s + 1)
            p7_B(s)
            if s + 1 < NT:
                p7_M(s + 1)
            if l < L - 1:
                to_xT(st7[s], s, 6 + (s % 2))

    P.final_wait("sp", [Xout])
    P.emit()
    return nc


def _consts():
    ident = np.eye(128, dtype=np.float32)
    p = np.arange(128)[:, None]
    f = np.arange(128)[None, :]
    mk = np.stack([(p <= f), (p >= f)]).astype(np.float32)
    cu = np.stack([(p <= f), (p >= f), np.ones((128, 128), bool)]).astype(np.float32)
    slopes = 2.0 ** (-8.0 * np.arange(1, 9) / 8)
    bias = np.zeros((2, 3, 128, 4, 128), np.float32)
    k = np.arange(128)[:, None]
    q = np.arange(128)[None, :]
    for kvh in range(2):
        for g in range(4):
            sl = slopes[kvh * 4 + g]
            for rel in range(3):
                d = k + (rel - 1) * 128 - q
                valid = np.abs(d) <= 128
                bias[kvh, rel, :, g, :] = np.where(valid, -8.0 * sl * np.abs(d), NEG)
    return ident, mk, cu, bias.reshape(2, 3, 128, 512)


def _prep_weights(w_in, b_gate, conv_w, conv_b, mlstm_norm_g, attn_sink, w_out, ln1_g, ln1_b,
                  w_ffn_in, w_ffn_out, ln2_g, ln2_b, w_ple_gate, w_ple_proj):
    Lh = w_in.shape[0]

    def kmaj(w):
        n = w.shape[-1]
        return np.ascontiguousarray(w.reshape(Lh, 8, 128, n).transpose(0, 2, 1, 3))

    cols = [w_in[:, :, c * 128:(c + 1) * 128] for c in range(8)]
    cols += [w_in[:, :, 2064 + c * 128:2064 + (c + 1) * 128] for c in range(4)]
    z = np.zeros((Lh, 1024, 64), np.float32)
    for kvh in range(2):
        kk = w_in[:, :, 2576 + kvh * 64:2576 + (kvh + 1) * 64]
        cols.append(np.concatenate([kk, z], axis=-1))
        cols.append(np.concatenate([z, kk], axis=-1))
    WF = np.stack([kmaj(c) for c in cols], axis=1)
    blk2 = np.concatenate([w_in[:, :, 2048:2064], w_in[:, :, 2704:2832], np.zeros((Lh, 1024, 512 - 144), np.float32)], axis=-1)
    WT = np.stack([kmaj(w_in[:, :, 1024:1536]), kmaj(w_in[:, :, 1536:2048]), kmaj(blk2)], axis=1)
    WO = kmaj(w_out)
    WFI = np.stack([kmaj(np.concatenate([w_ffn_in[:, :, c * 128:(c + 1) * 128], w_ffn_in[:, :, DFF + c * 128:DFF + (c + 1) * 128]], axis=-1))
                    for c in range(NFC)], axis=1)
    WFO = np.ascontiguousarray(w_ffn_out.reshape(Lh, NFC, 128, 1024).transpose(0, 2, 1, 3))
    WPG = kmaj(w_ple_gate)
    WPP = np.ascontiguousarray(w_ple_proj.reshape(Lh, 2, 128, 1024).transpose(0, 2, 1, 3))
    CW = np.ascontiguousarray(conv_w.reshape(Lh, 5, 8, 128).transpose(0, 3, 2, 1))
    CB = np.ascontiguousarray(conv_b.reshape(Lh, 8, 128).transpose(0, 2, 1))
    rep = lambda a: np.ascontiguousarray(np.broadcast_to(a[:, None, :], (Lh, 128, a.shape[-1])))
    LNP = np.stack([rep(ln1_g), rep(ln1_b), rep(ln2_g), rep(ln2_b)], axis=1)
    return dict(WF=WF, WT=WT, WO=WO, WFI=WFI, WFO=WFO, WPG=WPG, WPP=WPP, CW=CW, CB=CB,
                BG=rep(b_gate), NG=rep(mlstm_norm_g), SK=rep(attn_sink), LNP=LNP)


def make_in_maps(x, p, **w):
    ident, mk, cu, bias = _consts()
    shared = _prep_weights(**{k: np.asarray(v, np.float32) for k, v in w.items()})
    shared.update(CID=ident, CMK=mk, CU=cu, CBIAS=bias)
    x = np.asarray(x, np.float32)
    p = np.asarray(p, np.float32)
    maps = []
    for c in range(8):
        b = c % 4
        m = dict(shared)
        m["x"] = np.ascontiguousarray(x[b])
        m["xT"] = np.ascontiguousarray(x[b].T.reshape(8, 128, T).transpose(1, 0, 2))
        m["pT"] = np.ascontiguousarray(p[:, b].transpose(0, 2, 1).reshape(DEPTH, 2, 128, T).transpose(0, 2, 1, 3))
        maps.append(m)
    return maps


def kernel(x, p, w_in, b_gate, conv_w, conv_b, mlstm_norm_g, attn_sink, w_out, ln1_g, ln1_b,
           w_ffn_in, w_ffn_out, ln2_g, ln2_b, w_ple_gate, w_ple_proj):
    maps = make_in_maps(x, p, w_in=w_in, b_gate=b_gate, conv_w=conv_w, conv_b=conv_b, mlstm_norm_g=mlstm_norm_g,
                        attn_sink=attn_sink, w_out=w_out, ln1_g=ln1_g, ln1_b=ln1_b, w_ffn_in=w_ffn_in,
                        w_ffn_out=w_ffn_out, ln2_g=ln2_g, ln2_b=ln2_b, w_ple_gate=w_ple_gate, w_ple_proj=w_ple_proj)
    nc = build_nc()
    res = run_bass_kernel_spmd(nc, maps, core_ids=list(range(8)))
    out = np.stack([np.asarray(res.results[b]["y"], np.float32) for b in range(4)], axis=0)
    return out
```

```python
import bisect
import numpy as np
import concourse.bass as bass
import concourse.mybir as mybir
from concourse.bass_utils import run_bass_kernel_spmd

F32 = mybir.dt.float32
BF16 = mybir.dt.bfloat16
U8 = mybir.dt.uint8
AF = mybir.ActivationFunctionType
ALU = mybir.AluOpType
AX = mybir.AxisListType
ESZ = {F32: 4, BF16: 2, U8: 1}

D = 1024
T = 4096
NT = T // 128
DEPTH = 4
DFF = 2816
NFC = DFF // 128
ALPHA = (2 * DEPTH) ** 0.25
LN_EPS = 1e-5
MH_EPS = 1e-6
NEG = -240000.0


class IMap:
    def __init__(self):
        self.b = [0, 1 << 60]
        self.s = [[None, {}]]

    def _split(self, x):
        i = bisect.bisect_right(self.b, x) - 1
        if self.b[i] == x:
            return i
        w, r = self.s[i]
        self.b.insert(i + 1, x)
        self.s.insert(i + 1, [w, dict(r)])
        return i + 1

    def segs(self, lo, hi):
        i = self._split(lo)
        j = self._split(hi)
        return self.s[i:j]


class Op:
    __slots__ = ("eng", "fn", "waits", "dma", "idx", "ms", "rank")


class Prog:
    NS = 12

    def __init__(self, nc):
        self.nc = nc
        self.eng = {"pe": nc.tensor, "act": nc.scalar, "dve": nc.vector, "pool": nc.gpsimd, "sp": nc.sync}
        self.csem = {e: nc.alloc_semaphore("c_" + e) for e in ("pe", "act", "dve")}
        self.dsem = {q: [nc.alloc_semaphore("d_%s%d" % (q, j)) for j in range(self.NS)] for q in ("sp", "pool")}
        self.ops = []
        self.cnt = {e: 0 for e in self.eng}
        self.ndma = {"sp": 0, "pool": 0}
        self.seen = {e: {} for e in self.eng}
        self.maps = {}
        self.addr = {}

    def regions(self, ap):
        t = ap.tensor
        name = t.name
        pairs = list(ap.ap)
        off = ap.offset
        esz = ESZ[ap.dtype]
        if name in self.addr:
            space, base = self.addr[name]
            pstep = pairs[0][0]
            fo = off % pstep if pstep > 0 else off
            dims = pairs[1:]
        else:
            space, base = ("d", name), 0
            fo = off
            dims = pairs
        if space == "ps":
            hi = fo + 1 + sum((c - 1) * s for s, c in dims if c > 1 and s > 0)
            return space, [((fo * esz) // 2048 * 2048, ((hi * esz) + 2047) // 2048 * 2048)]
        dims = [(s, c) for (s, c) in dims if c > 1 and s != 0]
        if not dims:
            return space, [(base + fo * esz, base + (fo + 1) * esz)]
        inner = dims[-1]
        outer = dims[:-1]
        if inner[0] == 1:
            ilen = inner[1]
        else:
            ilen = (inner[1] - 1) * inner[0] + 1
        n = 1
        for s, c in outer:
            n *= c
        if n > 48 or space[0] == "d":
            hi = fo + ilen + sum((c - 1) * s for s, c in outer)
            return space, [(base + fo * esz, base + hi * esz)]
        starts = [fo]
        for s, c in outer:
            starts = [x + i * s for x in starts for i in range(c)]
        return space, [(base + x * esz, base + (x + ilen) * esz) for x in starts]

    def op(self, eng, fn, reads=(), writes=(), dma=None):
        o = Op()
        o.eng, o.fn, o.ms, o.rank = eng, fn, False, 0
        is_dma = (eng in ("sp", "pool")) if dma is None else dma
        self.cnt[eng] += 1
        o.idx = self.cnt[eng]
        deps = []
        rsegs, wsegs = [], []
        writes = list(writes) + [ap for ap in reads if ap.tensor.name == "ps"]
        reads = [ap for ap in reads if ap.tensor.name != "ps"]
        for ap in reads:
            sp, runs = self.regions(ap)
            m = self.maps.setdefault(sp, IMap())
            for lo, hi in runs:
                for sg in m.segs(lo, hi):
                    rsegs.append(sg)
                    if sg[0] is not None:
                        deps.append((sg[0], "RAW"))
        for ap in writes:
            sp, runs = self.regions(ap)
            m = self.maps.setdefault(sp, IMap())
            for lo, hi in runs:
                for sg in m.segs(lo, hi):
                    wsegs.append(sg)
                    if sg[0] is not None:
                        deps.append((sg[0], "WAW"))
                    for ev in sg[1].values():
                        deps.append((ev, "WAR"))
        if is_dma:
            n = self.ndma[eng]
            self.ndma[eng] += 1
            j, val = n % self.NS, 16 * (n // self.NS + 1)
            o.dma = (eng, j, val)
            if val > 16:
                deps.append((("d", eng, j, val - 16), "WAW"))
            myev = ("d", eng, j, val)
            mykey = ("d", eng, j)
        else:
            o.dma = None
            myev = ("c", eng, o.idx, o)
            mykey = eng
        waits = {}
        seen = self.seen[eng]
        for ev, kind in deps:
            if ev[0] == "c":
                src, val = ev[1], ev[2]
                if src == eng and (eng == "pe" or kind != "RAW"):
                    continue
                if seen.get(src, 0) >= val:
                    continue
                seen[src] = val
                ev[3].ms = True
                waits[src] = ev
            else:
                key = (ev[1], ev[2])
                if seen.get(key, 0) >= ev[3]:
                    continue
                seen[key] = ev[3]
                waits[key] = ev
        o.waits = list(waits.values())
        for sg in rsegs:
            sg[1][mykey] = myev
        for sg in wsegs:
            sg[0] = myev
            sg[1] = {}
        self.ops.append(o)
        return o

    def emit(self):
        rk = {e: 0 for e in self.csem}
        for o in self.ops:
            if o.ms:
                rk[o.eng] += 1
                o.rank = rk[o.eng]
        for o in self.ops:
            e = self.eng[o.eng]
            for ev in o.waits:
                if ev[0] == "c":
                    e.wait_ge(self.csem[ev[1]], ev[3].rank)
                else:
                    e.wait_ge(self.dsem[ev[1]][ev[2]], ev[3])
            ins = o.fn(e)
            if o.dma is not None:
                ins.then_inc(self.dsem[o.dma[0]][o.dma[1]], 16)
            elif o.ms:
                ins.then_inc(self.csem[o.eng], 1)

    def final_wait(self, eng, aps):
        def fn(e):
            return e.nop()
        return self.op(eng, fn, reads=aps, dma=False)


class Arena:
    def __init__(self, nc, P, nbytes):
        self.t = nc.alloc_sbuf_tensor("arena", [128, nbytes], U8)
        P.addr["arena"] = ("sb", 0)
        self.n = nbytes
        self.p = 0

    def alloc(self, shape, dt):
        n = ESZ[dt]
        for s in shape[1:]:
            n *= s
        off = (self.p + 63) // 64 * 64
        assert off + n <= self.n, ("SBUF arena overflow", off, n, self.n)
        self.p = off + n
        v = self.t[:, off:off + n].bitcast(dt)
        if len(shape) == 3:
            v = v.rearrange("p (a b) -> p a b", b=shape[2])
        elif len(shape) == 4:
            v = v.rearrange("p (a b c) -> p a b c", b=shape[2], c=shape[3])
        return v

    def ring(self, k, shape, dt):
        return [self.alloc(shape, dt) for _ in range(k)]


class Ring:
    def __init__(self, items):
        self.items = items
        self.i = 0

    def next(self):
        v = self.items[self.i % len(self.items)]
        self.i += 1
        return v


def build_nc(nlayers=DEPTH, debug=False):
    nc = bass.Bass("TRN2", target_bir_lowering=False)
    P = Prog(nc)

    def din(name, shape):
        return nc.dram_tensor(name, list(shape), F32, kind="ExternalInput").ap()

    def dscr(name, shape, dt):
        return nc.dram_tensor(name, list(shape), dt, kind=("ExternalOutput" if debug else "Internal")).ap()

    L = nlayers
    x_in = din("x", [T, D])
    xT_in = din("xT", [128, 8, T])
    pT_in = din("pT", [DEPTH, 128, 2, T])
    WF = din("WF", [DEPTH, 16, 128, 8, 128])
    WT = din("WT", [DEPTH, 3, 128, 8, 512])
    WO = din("WO", [DEPTH, 128, 8, 1024])
    WFI = din("WFI", [DEPTH, NFC, 128, 8, 256])
    WFO = din("WFO", [DEPTH, 128, NFC, 1024])
    WPG = din("WPG", [DEPTH, 128, 8, 1024])
    WPP = din("WPP", [DEPTH, 128, 2, 1024])
    CW = din("CW", [DEPTH, 128, 8, 5])
    CB = din("CB", [DEPTH, 128, 8])
    BG = din("BG", [DEPTH, 128, 16])
    NG = din("NG", [DEPTH, 128, 512])
    SK = din("SK", [DEPTH, 128, 8])
    LNP = din("LNP", [DEPTH, 4, 128, 1024])
    CID = din("CID", [128, 128])
    CMK = din("CMK", [2, 128, 128])
    CU = din("CU", [3, 128, 128])
    CBIAS = din("CBIAS", [2, 3, 128, 512])
    y_out = nc.dram_tensor("y", [T, D], F32, kind="ExternalOutput").ap()

    QKT = dscr("QKT", [8, 128, T], BF16)
    AQT = dscr("AQT", [4, 128, T], BF16)
    AKT = dscr("AKT", [4, 128, T], BF16)
    VS = dscr("VS", [T, 4, 129], BF16)
    SO = dscr("SO", [T, 512], F32)
    AVS = dscr("AVS", [T, 2, 65], BF16)
    HF = dscr("HF", [T, 512], F32)
    HS = dscr("HS", [T, 1024], BF16)
    X1 = dscr("X1", [T, D], F32)
    XB = [dscr("XB0", [T, D], F32), dscr("XB1", [T, D], F32)]
    ATS = dscr("ATS", [NT, 128, NFC, 128], BF16)

    A = Arena(nc, P, 211968)
    ps_t = nc.alloc_psum_tensor("ps", [128, 4096], F32)
    P.addr["ps"] = ("ps", 0)

    def bank(i, n=1):
        return ps_t[:, i * 512:(i + n) * 512]

    def bank_bf(i):
        return ps_t[:, i * 512:(i + 1) * 512].bitcast(BF16)

    xT = A.alloc([128, 8, T], BF16)
    ident = A.alloc([128, 128], BF16)
    mk = A.alloc([128, 2, 128], BF16)
    cu = A.alloc([128, 3, 128], F32)
    cbias = A.alloc([128, 6, 512], BF16)
    cw = A.alloc([128, 8, 5], F32)
    cb = A.alloc([128, 8], F32)
    bg = A.alloc([128, 16], F32)
    ng = A.alloc([128, 512], F32)
    sk = A.alloc([128, 8], F32)
    esk = A.alloc([128, 8], F32)
    GT = A.alloc([128, NT, 16], F32)
    LF = A.alloc([128, NT, 8], F32)
    TB = A.alloc([128, NT, 8], F32)
    TG = A.alloc([128, NT, 8], F32)
    EB = A.alloc([128, NT, 8], F32)
    WTl = A.alloc([128, NT, 8], F32)
    WHt = A.alloc([128, NT, 8], F32)
    EG = A.alloc([128, NT, 8], F32)
    phase_base = A.p

    P.op("pool", lambda e: e.dma_start(out=ident, in_=CID), writes=[ident])
    P.op("pool", lambda e: e.dma_start(out=mk, in_=CMK.rearrange("a p n -> p a n")), writes=[mk])
    P.op("sp", lambda e: e.dma_start(out=cu, in_=CU.rearrange("a p n -> p a n")), writes=[cu])
    P.op("pool", lambda e: e.dma_start(out=cbias, in_=CBIAS.rearrange("a b p n -> p (a b) n")), writes=[cbias])
    for kc in range(8):
        P.op("pool", lambda e, kc=kc: e.dma_start(out=xT[:, kc, :], in_=xT_in[:, kc, :]), writes=[xT[:, kc, :]])

    def ln_tile(y, stats, mv, rstd, gam, bet, eps):
        for h in range(2):
            P.op("dve", lambda e, h=h: e.bn_stats(out=stats[:, h, :], in_=y[:, h * 512:(h + 1) * 512]),
                 reads=[y[:, h * 512:(h + 1) * 512]], writes=[stats[:, h, :]])
        P.op("dve", lambda e: e.bn_aggr(out=mv, in_=stats), reads=[stats], writes=[mv])
        P.op("dve", lambda e: e.tensor_scalar(out=rstd, in0=mv[:, 1:2], scalar1=eps, scalar2=None, op0=ALU.add), reads=[mv], writes=[rstd])
        P.op("act", lambda e: e.activation(out=rstd, in_=rstd, func=AF.Ln), reads=[rstd], writes=[rstd])
        P.op("act", lambda e: e.activation(out=rstd, in_=rstd, func=AF.Exp, scale=-0.5), reads=[rstd], writes=[rstd])
        P.op("dve", lambda e: e.tensor_scalar(out=y, in0=y, scalar1=mv[:, 0:1], scalar2=rstd, op0=ALU.subtract, op1=ALU.mult),
             reads=[y, mv, rstd], writes=[y])
        P.op("dve", lambda e: e.tensor_tensor(out=y, in0=y, in1=gam, op=ALU.mult), reads=[y, gam], writes=[y])
        P.op("dve", lambda e: e.tensor_tensor(out=y, in0=y, in1=bet, op=ALU.add), reads=[y, bet], writes=[y])

    def to_bf(y, yb):
        P.op("act", lambda e: e.activation(out=yb, in_=y, func=AF.Copy), reads=[y], writes=[yb])

    def to_xT(yb, s, pb):
        pv = bank_bf(pb)
        for kc in range(8):
            P.op("pe", lambda e, kc=kc: e.transpose(out=pv[:, kc * 128:(kc + 1) * 128], in_=yb[:, kc * 128:(kc + 1) * 128], identity=ident),
                 reads=[yb[:, kc * 128:(kc + 1) * 128], ident], writes=[pv[:, kc * 128:(kc + 1) * 128]])
        dst = xT[:, :, s * 128:(s + 1) * 128]
        P.op("act", lambda e: e.activation(out=dst, in_=pv.rearrange("p (a b) -> p a b", b=128), func=AF.Copy),
             reads=[pv], writes=[dst])

    for l in range(L):
        Xin = x_in if l == 0 else XB[(l - 1) % 2]
        Xout = y_out if l == L - 1 else XB[l % 2]
        P.op("sp", lambda e, l=l: e.dma_start(out=cw, in_=CW[l]), writes=[cw])
        P.op("sp", lambda e, l=l: e.dma_start(out=cb, in_=CB[l]), writes=[cb])
        P.op("sp", lambda e, l=l: e.dma_start(out=bg, in_=BG[l]), writes=[bg])
        P.op("sp", lambda e, l=l: e.dma_start(out=ng, in_=NG[l]), writes=[ng])
        P.op("sp", lambda e, l=l: e.dma_start(out=sk, in_=SK[l]), writes=[sk])
        P.op("act", lambda e: e.activation(out=esk, in_=sk, func=AF.Exp), reads=[sk], writes=[esk])

        A.p = phase_base
        wf_r = Ring(A.ring(3, [128, 8, 128], BF16))
        cbuf_r = Ring(A.ring(2, [128, T + 4], F32))
        obuf_r = Ring(A.ring(2, [128, T], BF16))
        ct1_r = Ring(A.ring(4, [128, 512], F32))
        ct2_r = Ring(A.ring(2, [128, 512], F32))
        pb_r = Ring([0, 1, 2, 3])
        p1s = {}

        def p1_mm(cc):
            wf = wf_r.next()
            P.op("pool", lambda e, l=l: e.dma_start(out=wf, in_=WF[l, cc]), writes=[wf])
            is_m = cc < 8
            cbuf = None
            if is_m:
                cbuf = cbuf_r.next()
                P.op("dve", lambda e: e.memset(cbuf[:, 0:2], 0.0), writes=[cbuf[:, 0:2]])
                P.op("dve", lambda e: e.memset(cbuf[:, T + 2:T + 4], 0.0), writes=[cbuf[:, T + 2:T + 4]])
            obuf = obuf_r.next()
            for I in range(8):
                pb = bank(pb_r.next())
                for kc in range(8):
                    P.op("pe", lambda e, pb=pb, kc=kc, I=I: e.matmul(pb, lhsT=wf[:, kc, :], rhs=xT[:, kc, I * 512:(I + 1) * 512],
                                                                     start=(kc == 0), stop=(kc == 7)),
                         reads=[wf[:, kc, :], xT[:, kc, I * 512:(I + 1) * 512]], writes=[pb])
                if is_m:
                    dst = cbuf[:, 2 + I * 512:2 + (I + 1) * 512]
                    P.op("act", lambda e, dst=dst, pb=pb: e.activation(out=dst, in_=pb, func=AF.Copy), reads=[pb], writes=[dst])
                else:
                    dst = obuf[:, I * 512:(I + 1) * 512]
                    P.op("dve", lambda e, dst=dst, pb=pb: e.tensor_copy(out=dst, in_=pb), reads=[pb], writes=[dst])
            p1s[cc] = (cbuf, obuf)

        def p1_conv(cc):
            cbuf, obuf = p1s.pop(cc)
            if cc < 8:
                t1s = {}

                def tap0(I):
                    t1 = ct1_r.next()
                    s0 = cbuf[:, I * 512:I * 512 + 512]
                    P.op("act", lambda e: e.activation(out=t1, in_=s0, func=AF.Identity, bias=cb[:, cc:cc + 1], scale=cw[:, cc, 0:1]),
                         reads=[s0, cw, cb], writes=[t1])
                    t1s[I] = t1

                tap0(0)
                tap0(1)
                for I in range(8):
                    if I % 2 == 0:
                        for k in range(1, 5):
                            for I2 in (I, I + 1):
                                sk_ = cbuf[:, I2 * 512 + k:I2 * 512 + k + 512]
                                P.op("dve", lambda e, t1=t1s[I2], sk_=sk_, k=k: e.scalar_tensor_tensor(out=t1, in0=sk_, scalar=cw[:, cc, k:k + 1], in1=t1,
                                                                                                       op0=ALU.mult, op1=ALU.add),
                                     reads=[sk_, cw, t1s[I2]], writes=[t1s[I2]])
                        if I + 2 < 8:
                            tap0(I + 2)
                            tap0(I + 3)
                    t1 = t1s.pop(I)
                    t2 = ct2_r.next()
                    dst = obuf[:, I * 512:(I + 1) * 512]
                    if cc < 4:
                        P.op("act", lambda e, dst=dst, t1=t1: e.activation(out=dst, in_=t1, func=AF.Silu), reads=[t1], writes=[dst])
                    else:
                        P.op("act", lambda e, t2=t2, t1=t1: e.activation(out=t2, in_=t1, func=AF.Silu), reads=[t1], writes=[t2])
                        P.op("dve", lambda e, dst=dst, t2=t2: e.tensor_scalar(out=dst, in0=t2, scalar1=float(128 ** -0.5), scalar2=None, op0=ALU.mult),
                             reads=[t2], writes=[dst])
                dd = QKT[cc]
            elif cc < 12:
                dd = AQT[cc - 8]
            else:
                dd = AKT[cc - 12]
            P.op("sp", lambda e: e.dma_start(out=dd, in_=obuf), reads=[obuf], writes=[dd])

        p1_mm(0)
        for cc in range(16):
            if cc + 1 < 16:
                p1_mm(cc + 1)
            p1_conv(cc)

        A.p = phase_base
        wt_r = Ring(A.ring(2, [128, 8, 512], BF16))
        vst_r = Ring(A.ring(3, [128, 4, 129], BF16))
        sot_r = Ring(A.ring(3, [128, 512], F32))
        avt_r = Ring(A.ring(3, [128, 2, 65], BF16))
        for v in vst_r.items:
            P.op("dve", lambda e, v=v: e.memset(v[:, :, 128:129], 1.0), writes=[v[:, :, 128:129]])
        for v in avt_r.items:
            P.op("dve", lambda e, v=v: e.memset(v[:, :, 64:65], 1.0), writes=[v[:, :, 64:65]])
        pb_r = Ring([0, 1, 2, 3])
        for blk in range(3):
            wt = wt_r.next()
            P.op("pool", lambda e, wt=wt, l=l, blk=blk: e.dma_start(out=wt, in_=WT[l, blk]), writes=[wt])
            ncol = 512 if blk < 2 else 144
            for s in range(NT):
                pb = bank(pb_r.next())[:, 0:ncol]
                for kc in range(8):
                    P.op("pe", lambda e, pb=pb, wt=wt, kc=kc, s=s, ncol=ncol: e.matmul(pb, lhsT=xT[:, kc, s * 128:(s + 1) * 128], rhs=wt[:, kc, 0:ncol],
                                                                                        start=(kc == 0), stop=(kc == 7)),
                         reads=[wt[:, kc, 0:ncol], xT[:, kc, s * 128:(s + 1) * 128]], writes=[pb])
                if blk == 0:
                    st = vst_r.next()
                    P.op("act", lambda e, st=st, pb=pb: e.activation(out=st[:, :, 0:128], in_=pb.rearrange("p (a b) -> p a b", b=128), func=AF.Copy),
                         reads=[pb], writes=[st[:, :, 0:128]])
                    P.op("sp", lambda e, st=st, s=s: e.dma_start(out=VS[s * 128:(s + 1) * 128], in_=st), reads=[st], writes=[VS[s * 128:(s + 1) * 128]])
                elif blk == 1:
                    st = sot_r.next()
                    P.op("act", lambda e, st=st, pb=pb: e.activation(out=st, in_=pb, func=AF.Sigmoid), reads=[pb], writes=[st])
                    P.op("dve", lambda e, st=st: e.tensor_tensor(out=st, in0=st, in1=ng, op=ALU.mult), reads=[st, ng], writes=[st])
                    P.op("sp", lambda e, st=st, s=s: e.dma_start(out=SO[s * 128:(s + 1) * 128], in_=st), reads=[st], writes=[SO[s * 128:(s + 1) * 128]])
                else:
                    P.op("dve", lambda e, pb=pb, s=s: e.tensor_tensor(out=GT[:, s, :], in0=pb[:, 0:16], in1=bg, op=ALU.add),
                         reads=[pb, bg], writes=[GT[:, s, :]])
                    st = avt_r.next()
                    P.op("act", lambda e, st=st, pb=pb: e.activation(out=st[:, :, 0:64], in_=pb[:, 16:144].rearrange("p (a b) -> p a b", b=64), func=AF.Copy),
                         reads=[pb], writes=[st[:, :, 0:64]])
                    P.op("sp", lambda e, st=st, s=s: e.dma_start(out=AVS[s * 128:(s + 1) * 128], in_=st), reads=[st], writes=[AVS[s * 128:(s + 1) * 128]])

        A.p = phase_base
        tmpg = A.alloc([128, NT, 8], F32)
        P.op("act", lambda e: e.activation(out=tmpg, in_=GT[:, :, 8:16], func=AF.Exp, scale=-1.0), reads=[GT], writes=[tmpg])
        P.op("act", lambda e: e.activation(out=tmpg, in_=tmpg, func=AF.Ln, bias=1.0), reads=[tmpg], writes=[tmpg])
        P.op("dve", lambda e: e.tensor_scalar(out=LF, in0=tmpg, scalar1=-1.0, scalar2=None, op0=ALU.mult), reads=[tmpg], writes=[LF])
        pB = bank(4)[:, 0:NT * 8].rearrange("p (a b) -> p a b", b=8)
        pG = bank(5)[:, 0:NT * 8].rearrange("p (a b) -> p a b", b=8)
        for s in range(NT):
            for d_ in range(2):
                P.op("pe", lambda e, s=s, d_=d_: e.matmul(pB[:, s, d_ * 4:(d_ + 1) * 4], lhsT=cu[:, d_, :], rhs=LF[:, s, d_ * 4:(d_ + 1) * 4], start=True, stop=True),
                     reads=[cu, LF[:, s, :]], writes=[pB[:, s, d_ * 4:(d_ + 1) * 4]])
            P.op("pe", lambda e, s=s: e.matmul(pG[:, s, :], lhsT=cu[:, 2, :], rhs=LF[:, s, :], start=True, stop=True),
                 reads=[cu, LF[:, s, :]], writes=[pG[:, s, :]])
        P.op("dve", lambda e: e.tensor_copy(out=TB, in_=pB), reads=[pB], writes=[TB])
        P.op("dve", lambda e: e.tensor_copy(out=TG, in_=pG), reads=[pG], writes=[TG])
        P.op("act", lambda e: e.activation(out=EB, in_=TB, func=AF.Exp), reads=[TB], writes=[EB])
        P.op("act", lambda e: e.activation(out=EG, in_=TG, func=AF.Exp), reads=[TG], writes=[EG])
        EBI = LF
        P.op("act", lambda e: e.activation(out=EBI, in_=TB, func=AF.Exp, scale=-1.0), reads=[TB], writes=[EBI])
        P.op("dve", lambda e: e.tensor_tensor(out=tmpg, in0=GT[:, :, 0:8], in1=TB, op=ALU.subtract), reads=[GT, TB], writes=[tmpg])
        P.op("act", lambda e: e.activation(out=WTl, in_=tmpg, func=AF.Exp), reads=[tmpg], writes=[WTl])
        P.op("dve", lambda e: e.tensor_tensor(out=tmpg, in0=tmpg, in1=TG, op=ALU.add), reads=[tmpg, TG, WTl], writes=[tmpg])
        P.op("act", lambda e: e.activation(out=WHt, in_=tmpg, func=AF.Exp), reads=[tmpg], writes=[WHt])

        A.p = phase_base
        wo = A.alloc([128, 8, 1024], BF16)
        lng = A.alloc([128, 1024], F32)
        lnb = A.alloc([128, 1024], F32)
        for kc in range(0, 8, 4):
            P.op("pool", lambda e, l=l, kc=kc: e.dma_start(out=wo[:, kc:kc + 4, :], in_=WO[l, :, kc:kc + 4, :]), writes=[wo[:, kc:kc + 4, :]])
        P.op("sp", lambda e, l=l: e.dma_start(out=lng, in_=LNP[l, 0]), writes=[lng])
        P.op("sp", lambda e, l=l: e.dma_start(out=lnb, in_=LNP[l, 1]), writes=[lnb])
        p6_base = A.p
        aq_r = Ring(A.ring(2, [128, 4, 128], BF16))
        ak_r = A.ring(4, [128, 4, 128], BF16)
        av_r = A.ring(4, [128, 2, 65], BF16)
        pt_r = Ring(A.ring(5, [128, 512], BF16))
        ha_r = Ring(A.ring(2, [128, 8, 64], BF16))
        dn_r = Ring(A.ring(2, [128, 8], F32))
        ps_r = Ring([6])
        po_r = Ring([7])

        def load_kv(j):
            ak, av = ak_r[j % 4], av_r[j % 4]
            P.op("sp", lambda e: e.dma_start(out=ak, in_=AKT[:, :, j * 128:(j + 1) * 128].rearrange("c p t -> p c t")),
                 reads=[AKT[:, :, j * 128:(j + 1) * 128]], writes=[ak])
            P.op("sp", lambda e: e.dma_start(out=av, in_=AVS[j * 128:(j + 1) * 128]), reads=[AVS[j * 128:(j + 1) * 128]], writes=[av])

        aqs = {}

        def att_loads(s):
            if s + 1 < NT:
                load_kv(s + 1)
            aq = aq_r.next()
            P.op("sp", lambda e, aq=aq, s=s: e.dma_start(out=aq, in_=AQT[:, :, s * 128:(s + 1) * 128].rearrange("c p t -> p c t")),
                 reads=[AQT[:, :, s * 128:(s + 1) * 128]], writes=[aq])
            aqs[s] = aq

        load_kv(0)
        att_loads(0)

        def att_block(s):
            if s + 1 < NT:
                att_loads(s + 1)
            aq = aqs.pop(s)
            ha = ha_r.next()
            dn = dn_r.next()
            kbs = [j for j in (s - 1, s, s + 1) if 0 <= j < NT]
            for kvh in range(2):
                po = bank(po_r.next()).rearrange("p (a b) -> p a b", b=128)
                pts = []
                for j in kbs:
                    rel = j - s + 1
                    ak = ak_r[j % 4]
                    pb = bank(ps_r.next())
                    bt = cbias[:, kvh * 3 + rel, :]
                    P.op("pe", lambda e, pb=pb, bt=bt: e.matmul(pb, lhsT=ident, rhs=bt, start=True, stop=False), reads=[ident, bt], writes=[pb])
                    for g in range(4):
                        hd = 4 * kvh + g
                        P.op("pe", lambda e, pb=pb, ak=ak, aq=aq, g=g, hd=hd, kvh=kvh: e.matmul(pb[:, g * 128:(g + 1) * 128], lhsT=ak[:, kvh * 2 + (hd % 2), :],
                                                                                                  rhs=aq[:, hd // 2, :], start=False, stop=(g == 3)),
                             reads=[ak[:, kvh * 2 + (hd % 2), :], aq[:, hd // 2, :]], writes=[pb[:, g * 128:(g + 1) * 128]])
                    pt = pt_r.next()
                    P.op("act", lambda e, pt=pt, pb=pb: e.activation(out=pt, in_=pb, func=AF.Exp, scale=0.125), reads=[pb], writes=[pt])
                    pts.append((pt, j))
                    yield
                for g in range(4):
                    for n_, (pt, j) in enumerate(pts):
                        av = av_r[j % 4]
                        P.op("pe", lambda e, po=po, pt=pt, av=av, g=g, n_=n_, kvh=kvh, last=(n_ == len(pts) - 1): e.matmul(
                            po[:, g, 0:65], lhsT=pt[:, g * 128:(g + 1) * 128], rhs=av[:, kvh, :], start=(n_ == 0), stop=last),
                            reads=[pt[:, g * 128:(g + 1) * 128], av[:, kvh, :]], writes=[po[:, g, 0:65]])
                dk = dn[:, kvh * 4:(kvh + 1) * 4]
                P.op("dve", lambda e, dk=dk, po=po, kvh=kvh: e.tensor_tensor(out=dk, in0=po[:, :, 64], in1=esk[:, kvh * 4:(kvh + 1) * 4], op=ALU.add),
                     reads=[po, esk], writes=[dk])
                P.op("dve", lambda e, dk=dk: e.reciprocal(out=dk, in_=dk), reads=[dk], writes=[dk])
                hk = ha[:, kvh * 4:(kvh + 1) * 4, :]
                P.op("dve", lambda e, hk=hk, po=po, dk=dk: e.tensor_tensor(out=hk, in0=po[:, :, 0:64], in1=dk.unsqueeze(2).to_broadcast([128, 4, 64]), op=ALU.mult),
                     reads=[po, dk], writes=[hk])
            P.op("sp", lambda e, ha=ha, s=s: e.dma_start(out=HS[s * 128:(s + 1) * 128, 512:1024], in_=ha.rearrange("p a b -> p (a b)")),
                 reads=[ha], writes=[HS[s * 128:(s + 1) * 128, 512:1024]])

        qt_r = A.ring(4, [128, 4, 128], BF16)
        kt_r = A.ring(4, [128, 4, 128], BF16)
        vj_r = A.ring(4, [128, 4, 129], BF16)
        vt_r = A.ring(2, [128, 4, 129], BF16)
        ktk_r = A.ring(2, [128, 4, 128], BF16)
        at_r = A.ring(2, [128, 4, 128], BF16)
        Cn = A.alloc([128, 4, 129], F32)
        Cb3 = A.ring(3, [128, 4, 129], BF16)
        hd_r = A.ring(3, [128, 4, 128], F32)
        hf_r = A.ring(5, [128, 4, 128], F32)
        so_r = A.ring(5, [128, 4, 128], F32)
        sm_r = A.ring(3, [128, 32], F32)
        sq_r = A.ring(3, [128, 4, 128], F32)
        hb_r = A.ring(2, [128, 512], BF16)
        NG2 = 2 * NT

        def step_of(g):
            dr, n_ = g // NT, g % NT
            return dr, n_, (n_ if dr == 0 else NT - 1 - n_)

        def scan_loads2(g):
            dr, n_, j = step_of(g)
            tsl = slice(j * 128, (j + 1) * 128)
            hft, sot = hf_r[g % 5], so_r[g % 5]
            P.op("sp", lambda e: e.dma_start(out=hft.rearrange("p a b -> p (a b)"), in_=HF[tsl]), reads=[HF[tsl]], writes=[hft])
            P.op("sp", lambda e: e.dma_start(out=sot.rearrange("p a b -> p (a b)"), in_=SO[tsl]), reads=[SO[tsl]], writes=[sot])

        def scan_loads(g):
            dr, n_, j = step_of(g)
            tsl = slice(j * 128, (j + 1) * 128)
            qt, kt, vj = qt_r[g % 4], kt_r[g % 4], vj_r[g % 4]
            P.op("sp", lambda e: e.dma_start(out=qt, in_=QKT[0:4, :, tsl].rearrange("c p t -> p c t")), reads=[QKT[0:4, :, tsl]], writes=[qt])
            P.op("sp", lambda e: e.dma_start(out=kt, in_=QKT[4:8, :, tsl].rearrange("c p t -> p c t")), reads=[QKT[4:8, :, tsl]], writes=[kt])
            P.op("sp", lambda e: e.dma_start(out=vj, in_=VS[tsl]), reads=[VS[tsl]], writes=[vj])
            if dr == 1 and n_ >= 3:
                scan_loads2(g)

        def sx(g):
            dr, n_, j = step_of(g)
            qt, kt, vj = qt_r[g % 4], kt_r[g % 4], vj_r[g % 4]
            vt, ktk, at = vt_r[g % 2], ktk_r[g % 2], at_r[g % 2]
            cb_wr = Cb3[(g + 1) % 3]
            if n_ == 0:
                P.op("dve", lambda e: e.memset(Cn, 0.0), writes=[Cn])
                P.op("dve", lambda e: e.memset(Cb3[g % 3], 0.0), writes=[Cb3[g % 3]])
            wts = WTl[:, j, dr * 4:(dr + 1) * 4]
            egs = EG[:, j, dr * 4:(dr + 1) * 4]
            P.op("dve", lambda e: e.tensor_tensor(out=vt, in0=vj, in1=wts.unsqueeze(2).to_broadcast([128, 4, 129]), op=ALU.mult),
                 reads=[vj, wts], writes=[vt])
            pk = bank_bf(0)
            for h in range(4):
                P.op("pe", lambda e, h=h: e.transpose(out=pk[:, h * 128:(h + 1) * 128], in_=kt[:, h, :], identity=ident),
                     reads=[kt[:, h, :], ident], writes=[pk[:, h * 128:(h + 1) * 128]])
            P.op("dve", lambda e: e.tensor_copy(out=ktk, in_=pk[:, 0:512].rearrange("p (a b) -> p a b", b=128)),
                 reads=[pk[:, 0:512]], writes=[ktk])
            pS = bank(1)
            for h in range(4):
                P.op("pe", lambda e, h=h: e.matmul(pS[:, h * 128:(h + 1) * 128], lhsT=kt[:, h, :], rhs=qt[:, h, :], start=True, stop=True),
                     reads=[kt[:, h, :], qt[:, h, :]], writes=[pS[:, h * 128:(h + 1) * 128]])
            pC = bank(4, 2).rearrange("p (a b) -> p a b", b=256)
            for h in range(4):
                P.op("pe", lambda e, h=h: e.matmul(pC[:, h, 0:129], lhsT=ktk[:, h, :], rhs=vt[:, h, :], start=True, stop=True),
                     reads=[ktk[:, h, :], vt[:, h, :]], writes=[pC[:, h, 0:129]])
            P.op("dve", lambda e: e.tensor_tensor(out=Cn, in0=Cn, in1=pC[:, :, 0:129], op=ALU.add), reads=[Cn, pC], writes=[Cn])
            P.op("dve", lambda e: e.tensor_tensor(out=Cn, in0=Cn, in1=egs.unsqueeze(2).to_broadcast([128, 4, 129]), op=ALU.mult),
                 reads=[Cn, egs], writes=[Cn])
            P.op("act", lambda e: e.activation(out=cb_wr, in_=Cn, func=AF.Copy), reads=[Cn], writes=[cb_wr])
            P.op("dve", lambda e: e.tensor_tensor(out=at, in0=pS.rearrange("p (a b) -> p a b", b=128),
                                                  in1=mk[:, dr, :].unsqueeze(1).to_broadcast([128, 4, 128]), op=ALU.mult),
                 reads=[pS, mk], writes=[at])

        def sy(g):
            dr, n_, j = step_of(g)
            tsl = slice(j * 128, (j + 1) * 128)
            qt = qt_r[g % 4]
            vt, at = vt_r[g % 2], at_r[g % 2]
            cb_rd = Cb3[g % 3]
            ebi = EBI[:, j, dr * 4:(dr + 1) * 4]
            pN = bank(2, 2).rearrange("p (a b) -> p a b", b=256)
            for h in range(4):
                P.op("pe", lambda e, h=h: e.matmul(pN[:, h, 0:129], lhsT=at[:, h, :], rhs=vt[:, h, :], start=True, stop=False),
                     reads=[at[:, h, :], vt[:, h, :]], writes=[pN[:, h, 0:129]])
                P.op("pe", lambda e, h=h: e.matmul(pN[:, h, 0:129], lhsT=qt[:, h, :], rhs=cb_rd[:, h, :], start=False, stop=True),
                     reads=[qt[:, h, :], cb_rd[:, h, :]], writes=[pN[:, h, 0:129]])
            sm = sm_r[g % 3]
            t1, t2, rr = sm[:, 0:4], sm[:, 4:8], sm[:, 8:12]
            P.op("dve", lambda e: e.tensor_tensor(out=t1, in0=pN[:, :, 128], in1=ebi, op=ALU.max), reads=[pN, ebi], writes=[t1])
            P.op("dve", lambda e: e.scalar_tensor_tensor(out=t2, in0=pN[:, :, 128], scalar=-1.0, in1=t1, op0=ALU.mult, op1=ALU.max),
                 reads=[pN, t1], writes=[t2])
            P.op("dve", lambda e: e.reciprocal(out=rr, in_=t2), reads=[t2], writes=[rr])
            hdt = hd_r[g % 3]
            P.op("dve", lambda e: e.tensor_tensor(out=hdt, in0=pN[:, :, 0:128], in1=rr.unsqueeze(2).to_broadcast([128, 4, 128]), op=ALU.mult),
                 reads=[pN, rr], writes=[hdt])
            if dr == 0:
                P.op("sp", lambda e: e.dma_start(out=HF[tsl], in_=hdt.rearrange("p a b -> p (a b)")), reads=[hdt], writes=[HF[tsl]])
                return
            hft = hf_r[g % 5]
            P.op("dve", lambda e: e.tensor_tensor(out=hdt, in0=hdt, in1=hft, op=ALU.add), reads=[hdt, hft], writes=[hdt])

        def sy2(g):
            dr, n_, j = step_of(g)
            if dr == 0:
                return
            sm = sm_r[g % 3]
            hdt = hd_r[g % 3]
            ssum, ssq, mu, m2, var, rs = sm[:, 12:16], sm[:, 16:20], sm[:, 20:24], sm[:, 24:28], sm[:, 28:32], sm[:, 0:4]
            sq = sq_r[g % 3]
            P.op("dve", lambda e: e.memset(ssq, 0.0), writes=[ssq])
            P.op("dve", lambda e: e.tensor_reduce(out=ssum, in_=hdt, axis=AX.X, op=ALU.add), reads=[hdt], writes=[ssum])
            for h in range(4):
                P.op("act", lambda e, h=h: e.activation(out=sq[:, h, :], in_=hdt[:, h, :], func=AF.Square, accum_out=ssq[:, h:h + 1]),
                     reads=[hdt[:, h, :], ssq[:, h:h + 1]], writes=[sq[:, h, :], ssq[:, h:h + 1]])
            P.op("dve", lambda e: e.tensor_scalar(out=mu, in0=ssum, scalar1=1.0 / 128, scalar2=None, op0=ALU.mult), reads=[ssum], writes=[mu])
            P.op("dve", lambda e: e.tensor_tensor(out=m2, in0=mu, in1=mu, op=ALU.mult), reads=[mu], writes=[m2])
            P.op("dve", lambda e: e.tensor_scalar(out=var, in0=ssq, scalar1=1.0 / 128, scalar2=MH_EPS, op0=ALU.mult, op1=ALU.add),
                 reads=[ssq], writes=[var])
            P.op("dve", lambda e: e.tensor_tensor(out=var, in0=var, in1=m2, op=ALU.subtract), reads=[var, m2], writes=[var])
            P.op("act", lambda e: e.activation(out=rs, in_=var, func=AF.Ln), reads=[var], writes=[rs])
            P.op("act", lambda e: e.activation(out=rs, in_=rs, func=AF.Exp, scale=-0.5), reads=[rs], writes=[rs])

        def sz(g):
            dr, n_, j = step_of(g)
            if dr == 0:
                return
            tsl = slice(j * 128, (j + 1) * 128)
            sm = sm_r[g % 3]
            mu, rs, nb = sm[:, 20:24], sm[:, 0:4], sm[:, 4:8]
            hdt, sq, sot, hb = hd_r[g % 3], sq_r[g % 3], so_r[g % 5], hb_r[g % 2]
            P.op("dve", lambda e: e.scalar_tensor_tensor(out=nb, in0=mu, scalar=-1.0, in1=rs, op0=ALU.mult, op1=ALU.mult),
                 reads=[mu, rs], writes=[nb])
            for h in range(4):
                P.op("act", lambda e, h=h: e.activation(out=sq[:, h, :], in_=hdt[:, h, :], func=AF.Identity,
                                                        bias=nb[:, h:h + 1], scale=rs[:, h:h + 1]),
                     reads=[hdt[:, h, :], rs, nb], writes=[sq[:, h, :]])

        def sz2(g):
            dr, n_, j = step_of(g)
            if dr == 0:
                return
            tsl = slice(j * 128, (j + 1) * 128)
            sq, sot, hb = sq_r[g % 3], so_r[g % 5], hb_r[g % 2]
            P.op("dve", lambda e: e.tensor_tensor(out=hb.rearrange("p (a b) -> p a b", b=128), in0=sq, in1=sot, op=ALU.mult),
                 reads=[sq, sot], writes=[hb])
            P.op("sp", lambda e: e.dma_start(out=HS[tsl, 0:512], in_=hb), reads=[hb], writes=[HS[tsl, 0:512]])

        scan_loads(0)
        scan_loads(1)
        scan_loads(2)
        sx(0)
        def att_units():
            for s_ in range(NT):
                yield from att_block(s_)

        units = att_units()

        def att_step():
            try:
                next(units)
            except StopIteration:
                pass

        for g in range(NG2):
            dr, n_, j = step_of(g)
            if dr == 1 and n_ < 3:
                scan_loads2(g)
            att_step()
            if g + 1 < NG2:
                sx(g + 1)
            att_step()
            if g >= 2:
                sz(g - 2)
            sy(g)
            att_step()
            if g >= 1:
                sy2(g - 1)
            if g >= 2:
                sz2(g - 2)
            if g + 3 < NG2:
                scan_loads(g + 3)
        sy2(NG2 - 1)
        sz(NG2 - 2)
        sz2(NG2 - 2)
        sz(NG2 - 1)
        sz2(NG2 - 1)
        for _ in units:
            pass

        A.p = p6_base
        hin_r = Ring(A.ring(3, [128, 1024], BF16))
        xin_r = Ring(A.ring(3, [128, 1024], F32))
        hT_r = Ring(A.ring(2, [128, 8, 128], BF16))
        yb_r = Ring(A.ring(2, [128, 1024], BF16))
        st_r = Ring(A.ring(2, [128, 2, 6], F32))
        mv_r = Ring(A.ring(2, [128, 4], F32))
        st6 = {}

        ld6 = {}

        def p6_L(s):
            tsl = slice(s * 128, (s + 1) * 128)
            hin, xin = hin_r.next(), xin_r.next()
            P.op("sp", lambda e: e.dma_start(out=hin, in_=HS[tsl]), reads=[HS[tsl]], writes=[hin])
            P.op("sp", lambda e, Xin=Xin: e.dma_start(out=xin, in_=Xin[tsl]), reads=[Xin[tsl]], writes=[xin])
            ld6[s] = (hin, xin)

        def p6_A(s):
            tsl = slice(s * 128, (s + 1) * 128)
            hin, xin = ld6.pop(s)
            pv = bank_bf(6 + (s % 2))
            for kc in range(8):
                P.op("pe", lambda e, kc=kc: e.transpose(out=pv[:, kc * 128:(kc + 1) * 128], in_=hin[:, kc * 128:(kc + 1) * 128], identity=ident),
                     reads=[hin[:, kc * 128:(kc + 1) * 128], ident], writes=[pv[:, kc * 128:(kc + 1) * 128]])
            hT = hT_r.next()
            P.op("act", lambda e: e.activation(out=hT, in_=pv.rearrange("p (a b) -> p a b", b=128), func=AF.Copy), reads=[pv], writes=[hT])
            pm = bank(2 * (s % 2), 2)
            for n_ in range(2):
                for kc in range(8):
                    P.op("pe", lambda e, kc=kc, n_=n_: e.matmul(pm[:, n_ * 512:(n_ + 1) * 512], lhsT=hT[:, kc, :], rhs=wo[:, kc, n_ * 512:(n_ + 1) * 512],
                                                                start=(kc == 0), stop=(kc == 7)),
                         reads=[hT[:, kc, :], wo[:, kc, n_ * 512:(n_ + 1) * 512]], writes=[pm[:, n_ * 512:(n_ + 1) * 512]])
            st6[s] = (xin, pm, tsl)

        def p6_B(s):
            xin, pm, tsl = st6[s]
            P.op("dve", lambda e: e.scalar_tensor_tensor(out=xin, in0=xin, scalar=float(ALPHA), in1=pm, op0=ALU.mult, op1=ALU.add),
                 reads=[xin, pm], writes=[xin])
            mv = mv_r.next()
            ln_tile(xin, st_r.next(), mv[:, 0:2], mv[:, 2:3], lng, lnb, LN_EPS)
            P.op("sp", lambda e: e.dma_start(out=X1[tsl], in_=xin), reads=[xin], writes=[X1[tsl]])
            yb = yb_r.next()
            to_bf(xin, yb)
            st6[s] = yb

        p6_L(0)
        p6_L(1)
        p6_A(0)
        for s in range(NT):
            if s + 2 < NT:
                p6_L(s + 2)
            if s + 1 < NT:
                p6_A(s + 1)
            p6_B(s)
            to_xT(st6[s], s, 4 + (s % 2))

        A.p = phase_base
        wfi_r = Ring(A.ring(3, [128, 8, 256], BF16))
        sl_r = Ring(A.ring(3, [128, 512], F32))
        ao_r = Ring(A.ring(3, [128, 512], BF16))
        pb_r = Ring([0, 1, 2, 3, 4, 5, 6, 7])
        P7B_SMALL = 40 * 1024
        assert A.p <= phase_base + P7B_SMALL
        A.p = phase_base + P7B_SMALL
        wpg = A.alloc([128, 8, 1024], BF16)
        wpp = A.alloc([128, 2, 1024], BF16)
        pT = A.alloc([128, 2, T], BF16)
        wfo = A.alloc([128, NFC, 1024], BF16)
        w7 = []
        for kc in range(0, 8, 4):
            w7.append(lambda l=l, kc=kc: P.op("pool", lambda e: e.dma_start(out=wpg[:, kc:kc + 4, :], in_=WPG[l, :, kc:kc + 4, :]), writes=[wpg[:, kc:kc + 4, :]]))
        w7.append(lambda l=l: P.op("pool", lambda e: e.dma_start(out=wpp, in_=WPP[l]), writes=[wpp]))
        for kc in range(2):
            w7.append(lambda l=l, kc=kc: P.op("pool", lambda e: e.dma_start(out=pT[:, kc, :], in_=pT_in[l, :, kc, :]), writes=[pT[:, kc, :]]))
        for c0 in range(0, NFC, 4):
            c1 = min(NFC, c0 + 4)
            w7.append(lambda l=l, c0=c0, c1=c1: P.op("pool", lambda e: e.dma_start(out=wfo[:, c0:c1, :], in_=WFO[l, :, c0:c1, :]), writes=[wfo[:, c0:c1, :]]))
        for c in range(NFC):
            if c >= 3 and w7:
                w7.pop(0)()
            wfi = wfi_r.next()
            P.op("pool", lambda e, wfi=wfi, l=l, c=c: e.dma_start(out=wfi, in_=WFI[l, c]), writes=[wfi])
            for I in range(8):
                pg, pu = bank(pb_r.next()), bank(pb_r.next())
                for half, pb in ((0, pg), (1, pu)):
                    for kc in range(8):
                        P.op("pe", lambda e, pb=pb, wfi=wfi, kc=kc, I=I, half=half: e.matmul(pb, lhsT=wfi[:, kc, half * 128:(half + 1) * 128],
                                                                                              rhs=xT[:, kc, I * 512:(I + 1) * 512], start=(kc == 0), stop=(kc == 7)),
                             reads=[wfi[:, kc, half * 128:(half + 1) * 128], xT[:, kc, I * 512:(I + 1) * 512]], writes=[pb])
                sl, ao = sl_r.next(), ao_r.next()
                P.op("act", lambda e, sl=sl, pg=pg: e.activation(out=sl, in_=pg, func=AF.Silu), reads=[pg], writes=[sl])
                P.op("dve", lambda e, ao=ao, sl=sl, pu=pu: e.tensor_tensor(out=ao, in0=sl, in1=pu, op=ALU.mult), reads=[sl, pu], writes=[ao])
                dst = ATS[4 * I:4 * I + 4, :, c, :].rearrange("s p j -> p s j")
                P.op("sp", lambda e, dst=dst, ao=ao: e.dma_start(out=dst, in_=ao.rearrange("p (s j) -> p s j", j=128)), reads=[ao], writes=[dst])

        while w7:
            w7.pop(0)()

        A.p = phase_base
        lng2 = A.alloc([128, 1024], F32)
        lnb2 = A.alloc([128, 1024], F32)
        ai_r = Ring(A.ring(2, [128, NFC, 128], BF16))
        x1_r = Ring(A.ring(2, [128, 1024], F32))
        sg_r = Ring(A.ring(2, [128, 1024], F32))
        yb_r = Ring(A.ring(2, [128, 1024], BF16))
        st_r = Ring(A.ring(2, [128, 2, 6], F32))
        mv_r = Ring(A.ring(2, [128, 4], F32))
        assert A.p <= phase_base + P7B_SMALL, (A.p - phase_base)
        P.op("sp", lambda e, l=l: e.dma_start(out=lng2, in_=LNP[l, 2]), writes=[lng2])
        P.op("sp", lambda e, l=l: e.dma_start(out=lnb2, in_=LNP[l, 3]), writes=[lnb2])
        st7 = {}

        ld7a, ld7x = {}, {}

        def p7_La(s):
            ai = ai_r.next()
            P.op("sp", lambda e: e.dma_start(out=ai, in_=ATS[s]), reads=[ATS[s]], writes=[ai])
            ld7a[s] = ai

        def p7_Lx(s):
            tsl = slice(s * 128, (s + 1) * 128)
            x1 = x1_r.next()
            P.op("sp", lambda e: e.dma_start(out=x1, in_=X1[tsl]), reads=[X1[tsl]], writes=[x1])
            ld7x[s] = x1

        def p7_A(s):
            tsl = slice(s * 128, (s + 1) * 128)
            ai, x1 = ld7a.pop(s), ld7x.pop(s)
            pgp = bank(0, 2)
            ppp = bank(2, 2)
            pff = bank(4, 2)
            for n_ in range(2):
                cs = slice(n_ * 512, (n_ + 1) * 512)
                for kc in range(8):
                    P.op("pe", lambda e, kc=kc, cs=cs: e.matmul(pgp[:, cs], lhsT=xT[:, kc, tsl], rhs=wpg[:, kc, cs], start=(kc == 0), stop=(kc == 7)),
                         reads=[xT[:, kc, tsl], wpg[:, kc, cs]], writes=[pgp[:, cs]])
                for kc in range(2):
                    P.op("pe", lambda e, kc=kc, cs=cs: e.matmul(ppp[:, cs], lhsT=pT[:, kc, tsl], rhs=wpp[:, kc, cs], start=(kc == 0), stop=(kc == 1)),
                         reads=[pT[:, kc, tsl], wpp[:, kc, cs]], writes=[ppp[:, cs]])
            sg = sg_r.next()
            P.op("act", lambda e: e.activation(out=sg, in_=pgp, func=AF.Sigmoid), reads=[pgp], writes=[sg])
            st7[s] = (x1, sg, pff, tsl, ai)
            st7[("ppp", s)] = ppp

        def p7_A2(s):
            x1, sg, pff, tsl, ai = st7[s]

            for n_ in range(2):
                cs = slice(n_ * 512, (n_ + 1) * 512)
                for c in range(NFC):
                    P.op("pe", lambda e, c=c, cs=cs: e.matmul(pff[:, cs], lhsT=ai[:, c, :], rhs=wfo[:, c, cs], start=(c == 0), stop=(c == NFC - 1)),
                         reads=[ai[:, c, :], wfo[:, c, cs]], writes=[pff[:, cs]])

        def p7_M(s):
            x1, sg, pff, tsl, ai = st7[s]
            ppp = st7.pop(("ppp", s))
            P.op("dve", lambda e: e.tensor_tensor(out=sg, in0=sg, in1=ppp, op=ALU.mult), reads=[sg, ppp], writes=[sg])

        def p7_Bx(s):
            x1, sg, pff, tsl, ai = st7[s]
            P.op("dve", lambda e: e.scalar_tensor_tensor(out=x1, in0=x1, scalar=float(ALPHA), in1=pff, op0=ALU.mult, op1=ALU.add),
                 reads=[x1, pff], writes=[x1])

        def p7_B(s):
            x1, sg, pff, tsl, ai = st7[s]
            P.op("dve", lambda e: e.tensor_tensor(out=x1, in0=x1, in1=sg, op=ALU.add), reads=[x1, sg], writes=[x1])
            mv = mv_r.next()
            ln_tile(x1, st_r.next(), mv[:, 0:2], mv[:, 2:3], lng2, lnb2, LN_EPS)
            P.op("sp", lambda e, Xout=Xout: e.dma_start(out=Xout[tsl], in_=x1), reads=[x1], writes=[Xout[tsl]])
            if l < L - 1:
                yb = yb_r.next()
                to_bf(x1, yb)
                st7[s] = yb

        p7_La(0)
        p7_La(1)
        p7_Lx(0)
        p7_A(0)
        p7_M(0)
        p7_A2(0)
        for s in range(NT):
            if s + 2 < NT:
                p7_La(s + 2)
            if s + 1 < NT:
                p7_Lx(s + 1)
            if s + 1 < NT:
                p7_A(s + 1)
            p7_Bx(s)
            if s + 1 < NT:
                p7_A2(s + 1)
            p7_B(s)
            if s + 1 < NT:
                p7_M(s + 1)
            if l < L - 1:
                to_xT(st7[s], s, 6 + (s % 2))

    P.final_wait("sp", [Xout])
    P.emit()
    return nc


def _consts():
    ident = np.eye(128, dtype=np.float32)
    p = np.arange(128)[:, None]
    f = np.arange(128)[None, :]
    mk = np.stack([(p <= f), (p >= f)]).astype(np.float32)
    cu = np.stack([(p <= f), (p >= f), np.ones((128, 128), bool)]).astype(np.float32)
    slopes = 2.0 ** (-8.0 * np.arange(1, 9) / 8)
    bias = np.zeros((2, 3, 128, 4, 128), np.float32)
    k = np.arange(128)[:, None]
    q = np.arange(128)[None, :]
    for kvh in range(2):
        for g in range(4):
            sl = slopes[kvh * 4 + g]
            for rel in range(3):
                d = k + (rel - 1) * 128 - q
                valid = np.abs(d) <= 128
                bias[kvh, rel, :, g, :] = np.where(valid, -8.0 * sl * np.abs(d), NEG)
    return ident, mk, cu, bias.reshape(2, 3, 128, 512)


def _prep_weights(w_in, b_gate, conv_w, conv_b, mlstm_norm_g, attn_sink, w_out, ln1_g, ln1_b,
                  w_ffn_in, w_ffn_out, ln2_g, ln2_b, w_ple_gate, w_ple_proj):
    Lh = w_in.shape[0]

    def kmaj(w):
        n = w.shape[-1]
        return np.ascontiguousarray(w.reshape(Lh, 8, 128, n).transpose(0, 2, 1, 3))

    cols = [w_in[:, :, c * 128:(c + 1) * 128] for c in range(8)]
    cols += [w_in[:, :, 2064 + c * 128:2064 + (c + 1) * 128] for c in range(4)]
    z = np.zeros((Lh, 1024, 64), np.float32)
    for kvh in range(2):
        kk = w_in[:, :, 2576 + kvh * 64:2576 + (kvh + 1) * 64]
        cols.append(np.concatenate([kk, z], axis=-1))
        cols.append(np.concatenate([z, kk], axis=-1))
    WF = np.stack([kmaj(c) for c in cols], axis=1)
    blk2 = np.concatenate([w_in[:, :, 2048:2064], w_in[:, :, 2704:2832], np.zeros((Lh, 1024, 512 - 144), np.float32)], axis=-1)
    WT = np.stack([kmaj(w_in[:, :, 1024:1536]), kmaj(w_in[:, :, 1536:2048]), kmaj(blk2)], axis=1)
    WO = kmaj(w_out)
    WFI = np.stack([kmaj(np.concatenate([w_ffn_in[:, :, c * 128:(c + 1) * 128], w_ffn_in[:, :, DFF + c * 128:DFF + (c + 1) * 128]], axis=-1))
                    for c in range(NFC)], axis=1)
    WFO = np.ascontiguousarray(w_ffn_out.reshape(Lh, NFC, 128, 1024).transpose(0, 2, 1, 3))
    WPG = kmaj(w_ple_gate)
    WPP = np.ascontiguousarray(w_ple_proj.reshape(Lh, 2, 128, 1024).transpose(0, 2, 1, 3))
    CW = np.ascontiguousarray(conv_w.reshape(Lh, 5, 8, 128).transpose(0, 3, 2, 1))
    CB = np.ascontiguousarray(conv_b.reshape(Lh, 8, 128).transpose(0, 2, 1))
    rep = lambda a: np.ascontiguousarray(np.broadcast_to(a[:, None, :], (Lh, 128, a.shape[-1])))
    LNP = np.stack([rep(ln1_g), rep(ln1_b), rep(ln2_g), rep(ln2_b)], axis=1)
    return dict(WF=WF, WT=WT, WO=WO, WFI=WFI, WFO=WFO, WPG=WPG, WPP=WPP, CW=CW, CB=CB,
                BG=rep(b_gate), NG=rep(mlstm_norm_g), SK=rep(attn_sink), LNP=LNP)


def make_in_maps(x, p, **w):
    ident, mk, cu, bias = _consts()
    shared = _prep_weights(**{k: np.asarray(v, np.float32) for k, v in w.items()})
    shared.update(CID=ident, CMK=mk, CU=cu, CBIAS=bias)
    x = np.asarray(x, np.float32)
    p = np.asarray(p, np.float32)
    maps = []
    for c in range(8):
        b = c % 4
        m = dict(shared)
        m["x"] = np.ascontiguousarray(x[b])
        m["xT"] = np.ascontiguousarray(x[b].T.reshape(8, 128, T).transpose(1, 0, 2))
        m["pT"] = np.ascontiguousarray(p[:, b].transpose(0, 2, 1).reshape(DEPTH, 2, 128, T).transpose(0, 2, 1, 3))
        maps.append(m)
    return maps


def kernel(x, p, w_in, b_gate, conv_w, conv_b, mlstm_norm_g, attn_sink, w_out, ln1_g, ln1_b,
           w_ffn_in, w_ffn_out, ln2_g, ln2_b, w_ple_gate, w_ple_proj):
    maps = make_in_maps(x, p, w_in=w_in, b_gate=b_gate, conv_w=conv_w, conv_b=conv_b, mlstm_norm_g=mlstm_norm_g,
                        attn_sink=attn_sink, w_out=w_out, ln1_g=ln1_g, ln1_b=ln1_b, w_ffn_in=w_ffn_in,
                        w_ffn_out=w_ffn_out, ln2_g=ln2_g, ln2_b=ln2_b, w_ple_gate=w_ple_gate, w_ple_proj=w_ple_proj)
    nc = build_nc()
    res = run_bass_kernel_spmd(nc, maps, core_ids=list(range(8)))
    out = np.stack([np.asarray(res.results[b]["y"], np.float32) for b in range(4)], axis=0)
    return out
```
